# Optimizing a Trainium2 kernel written in Bass

```python
import jax
import jax.numpy as jnp
from jax import lax
import numpy as np

D_MODEL = 2048
BATCH = 2
SEQ = 8192
DEPTH = 1

GRID_W = 64
CTX_LEN = 256
D_MIX = D_MODEL
D_ATTN = D_MIX // 2
D_RWKV = D_MIX - D_ATTN
HEAD_DIM = 64
N_HEADS_A = D_ATTN // HEAD_DIM
N_KV_HEADS = 4
GQA_GROUP = N_HEADS_A // N_KV_HEADS
WINDOW = 128
BLOCK = 128
BAND = BLOCK + 2 * WINDOW
ROPE_BASE = 10000.0
RWKV_HEAD = 64
N_HEADS_R = D_RWKV // RWKV_HEAD
D_DECAY_LORA = max(32, int(round(1.8 * D_RWKV ** 0.5 / 32)) * 32)
D_AAA_LORA = max(32, int(round(1.8 * D_RWKV ** 0.5 / 32)) * 32)
D_GATE_LORA = max(32, int(round(0.6 * D_RWKV ** 0.8 / 32)) * 32)
D_FF = -(-(8 * D_MODEL) // (3 * 256)) * 256
NORM_EPS = 1e-6
LNX_EPS = 64e-5
MASK_VALUE = -1e30
ATTN_SPLITS = (D_ATTN, N_KV_HEADS * HEAD_DIM, N_KV_HEADS * HEAD_DIM)
RWKV_SPLITS = (D_RWKV, D_RWKV, D_RWKV, D_DECAY_LORA, D_DECAY_LORA, D_AAA_LORA, D_AAA_LORA, D_GATE_LORA)
D_IN_ATTN = sum(ATTN_SPLITS)
D_IN_RWKV = sum(RWKV_SPLITS)
D_IN = D_IN_ATTN + D_IN_RWKV
F32 = jnp.float32

kernel_name = 'hybrid_swa_rwkv7_dit_block'


def _split(z, sizes):
    return jnp.split(z, [int(s) for s in np.cumsum(sizes)[:-1]], axis=-1)


def rmsnorm(x, g):
    xf = x.astype(F32)
    y = xf * lax.rsqrt(jnp.mean(xf * xf, axis=-1, keepdims=True) + NORM_EPS)
    return (y * g.astype(F32)).astype(x.dtype)


def modulate(h, shift, scale):
    return h * (1 + scale) + shift


def swiglu(h, w1, w3, w2):
    return (jax.nn.silu(h @ w1) * (h @ w3)) @ w2


def axial_rope(x, row, col):
    half = HEAD_DIM // 2
    nf = half // 2
    freqs = 1.0 / (ROPE_BASE ** (jnp.arange(nf, dtype=F32) / nf))

    def rot(xa, pos):
        ang = pos.astype(F32)[:, None] * freqs[None, :]
        cos = jnp.cos(ang)[None, :, None, :]
        sin = jnp.sin(ang)[None, :, None, :]
        x1, x2 = xa[..., :nf], xa[..., nf:]
        return jnp.concatenate([x1 * cos - x2 * sin, x2 * cos + x1 * sin], axis=-1)

    xf = x.astype(F32)
    return jnp.concatenate([rot(xf[..., :half], row), rot(xf[..., half:], col)], axis=-1).astype(x.dtype)


def sink_softmax(s, sink_hg):
    sk = sink_hg[None, :, :, None, None]
    m = jnp.maximum(jnp.max(s, axis=-1, keepdims=True), sk)
    p = jnp.exp(s - m)
    return p / (jnp.sum(p, axis=-1, keepdims=True) + jnp.exp(sk - m))


def windowed_gqa(q, k, v, k_ctx, v_ctx, sink):
    B, T = q.shape[0], q.shape[1]
    nb = T // BLOCK
    scale = HEAD_DIM ** -0.5
    qb = (q.astype(F32) * scale).reshape(B, nb, BLOCK, N_KV_HEADS, GQA_GROUP, HEAD_DIM)
    qb = qb.transpose(1, 0, 2, 3, 4, 5)
    pad = ((0, 0), (WINDOW, WINDOW), (0, 0), (0, 0))
    kp = jnp.pad(k.astype(F32), pad)
    vp = jnp.pad(v.astype(F32), pad)
    kc = k_ctx.astype(F32)
    vc = v_ctx.astype(F32)
    sink_hg = sink.astype(F32).reshape(N_KV_HEADS, GQA_GROUP)
    offs_q = jnp.arange(BLOCK)
    offs_k = jnp.arange(BAND) - WINDOW
    in_window = jnp.abs(offs_q[:, None] - offs_k[None, :]) <= WINDOW

    def one_block(args):
        qi, bi = args
        start = bi * BLOCK
        kb = lax.dynamic_slice_in_dim(kp, start, BAND, axis=1)
        vb = lax.dynamic_slice_in_dim(vp, start, BAND, axis=1)
        kpos = start + offs_k
        valid = in_window & ((kpos >= 0) & (kpos < T))[None, :]
        s_band = jnp.where(valid, jnp.einsum('bqhgd,bkhd->bhgqk', qi, kb), MASK_VALUE)
        s_ctx = jnp.einsum('bqhgd,bkhd->bhgqk', qi, kc)
        p = sink_softmax(jnp.concatenate([s_band, s_ctx], axis=-1), sink_hg)
        return jnp.einsum('bhgqk,bkhd->bqhgd', p, jnp.concatenate([vb, vc], axis=1))

    out = lax.map(one_block, (qb, jnp.arange(nb)))
    return out.transpose(1, 0, 2, 3, 4, 5).reshape(B, T, D_ATTN).astype(q.dtype)


def context_gqa(q, k, v, sink):
    B, C = q.shape[0], q.shape[1]
    qf = (q.astype(F32) * HEAD_DIM ** -0.5).reshape(B, C, N_KV_HEADS, GQA_GROUP, HEAD_DIM)
    s = jnp.einsum('bqhgd,bkhd->bhgqk', qf, k.astype(F32))
    p = sink_softmax(s, sink.astype(F32).reshape(N_KV_HEADS, GQA_GROUP))
    return jnp.einsum('bhgqk,bkhd->bqhgd', p, v.astype(F32)).reshape(B, C, D_ATTN).astype(q.dtype)


def token_shift(h, mu_prev, mu_next):
    prev = jnp.pad(h, ((0, 0), (1, 0), (0, 0)))[:, :-1]
    nxt = jnp.pad(h, ((0, 0), (0, 1), (0, 0)))[:, 1:]
    return h + mu_prev * (prev - h) + mu_next * (nxt - h)


def _heads(t):
    return t.reshape(t.shape[0], t.shape[1], N_HEADS_R, RWKV_HEAD)


def rwkv_prepare(z, mu_prev, mu_next, w0, w_up, a0, a_up, k_k, k_a, g_up):
    zs = token_shift(z.astype(F32), mu_prev, mu_next)
    r, k, v, wdf, wdb, adf, adb, gd = _split(zs, RWKV_SPLITS)
    kk = _heads(k * k_k)
    kk = kk * lax.rsqrt(jnp.maximum(jnp.sum(kk * kk, axis=-1, keepdims=True), 1e-12))
    dirs = []
    for d, (wd, ad) in enumerate(((wdf, adf), (wdb, adb))):
        wlog = -jax.nn.softplus(-(w0[d] + jnp.tanh(wd) @ w_up[d])) - 0.5
        decay = jnp.exp(-jnp.exp(wlog))
        a = jax.nn.sigmoid(a0[d] + ad @ a_up[d])
        kd = k * (1 + (a - 1) * k_a)
        dirs.append((_heads(decay), _heads(kd), _heads(a)))
    gate = jax.nn.sigmoid(gd) @ g_up
    return _heads(r), _heads(v), kk, dirs, gate


def wkv_scan(state0, r, w, k, v, kk, a, reverse):
    tm = lambda t: jnp.swapaxes(t, 0, 1)
    xs = (tm(r), tm(w), tm(k), tm(v), tm(-kk), tm(kk * a))

    def step(S, inp):
        r_t, w_t, k_t, v_t, nkk_t, b_t = inp
        sa = jnp.einsum('bhvk,bhk->bhv', S, nkk_t)
        S = S * w_t[:, :, None, :] + sa[..., None] * b_t[:, :, None, :] + v_t[..., None] * k_t[:, :, None, :]
        return S, jnp.einsum('bhvk,bhk->bhv', S, r_t)

    S_final, ys = lax.scan(step, state0, xs, reverse=reverse)
    return S_final, jnp.swapaxes(ys, 0, 1)


def rwkv_output(y, r, v, dirs, gate, r_k, lnx_g, lnx_b):
    B, T = y.shape[0], y.shape[1]
    mean = jnp.mean(y, axis=-1, keepdims=True)
    var = jnp.mean(jnp.square(y - mean), axis=-1, keepdims=True)
    yn = ((y - mean) * lax.rsqrt(var + LNX_EPS)).reshape(B, T, D_RWKV) * lnx_g + lnx_b
    rk = r_k.reshape(N_HEADS_R, RWKV_HEAD)
    bonus = sum(jnp.sum(r * kd * rk, axis=-1, keepdims=True) * v for _, kd, _ in dirs)
    return (yn + bonus.reshape(B, T, D_RWKV)) * gate


def hybrid_layer(x, xc, c, c_ctx, row, col, ada_w, ada_b, norm1_g, norm2_g, w_in, attn_sink,
                 ts_prev, ts_next, w0, w_up, a0, a_up, k_k, k_a, r_k, g_up, lnx_g, lnx_b,
                 w_out, ffn_w1, ffn_w3, ffn_w2, update_ctx):
    B, T = x.shape[0], x.shape[1]
    C = xc.shape[1]
    mod = jax.nn.silu(c) @ ada_w + ada_b
    mod_c = jax.nn.silu(c_ctx) @ ada_w + ada_b
    sh1, sc1, g1, sh2, sc2, g2 = jnp.split(mod[:, None, :], 6, axis=-1)
    sh1c, sc1c, g1c, sh2c, sc2c, g2c = jnp.split(mod_c, 6, axis=-1)

    z = modulate(rmsnorm(x, norm1_g), sh1, sc1) @ w_in
    zc = modulate(rmsnorm(xc, norm1_g), sh1c, sc1c) @ w_in
    za, zr = z[..., :D_IN_ATTN], z[..., D_IN_ATTN:]
    zca, zcr = zc[..., :D_IN_ATTN], zc[..., D_IN_ATTN:]

    q, k, v = _split(za, ATTN_SPLITS)
    q = axial_rope(q.reshape(B, T, N_HEADS_A, HEAD_DIM), row, col)
    k = axial_rope(k.reshape(B, T, N_KV_HEADS, HEAD_DIM), row, col)
    v = v.reshape(B, T, N_KV_HEADS, HEAD_DIM)
    qc, kc, vc = _split(zca, ATTN_SPLITS)
    kc = kc.reshape(B, C, N_KV_HEADS, HEAD_DIM)
    vc = vc.reshape(B, C, N_KV_HEADS, HEAD_DIM)
    attn = windowed_gqa(q, k, v, kc, vc, attn_sink)

    rw_args = (ts_prev, ts_next, w0, w_up, a0, a_up, k_k, k_a, g_up)
    r, vr, kk, dirs, gate = rwkv_prepare(zr, *rw_args)
    rc, vrc, kkc, dirsc, gatec = rwkv_prepare(zcr, *rw_args)
    S0 = jnp.zeros((B, N_HEADS_R, RWKV_HEAD, RWKV_HEAD), F32)
    y_lat = jnp.zeros_like(r)
    yc_dirs = []
    for d in range(2):
        rev = d == 1
        wd, kd, ad = dirs[d]
        wdc, kdc, adc = dirsc[d]
        Sc, yc_d = wkv_scan(S0, rc, wdc, kdc, vrc, kkc, adc, rev)
        _, y_d = wkv_scan(Sc, r, wd, kd, vr, kk, ad, rev)
        y_lat = y_lat + y_d
        yc_dirs.append(yc_d)
    rw = rwkv_output(y_lat, r, vr, dirs, gate, r_k, lnx_g, lnx_b).astype(x.dtype)

    x = x + g1 * (jnp.concatenate([attn, rw], axis=-1) @ w_out)
    x = x + g2 * swiglu(modulate(rmsnorm(x, norm2_g), sh2, sc2), ffn_w1, ffn_w3, ffn_w2)

    if update_ctx:
        attn_c = context_gqa(qc.reshape(B, C, N_HEADS_A, HEAD_DIM), kc, vc, attn_sink)
        rw_c = rwkv_output(yc_dirs[0] + yc_dirs[1], rc, vrc, dirsc, gatec, r_k, lnx_g, lnx_b).astype(xc.dtype)
        xc = xc + g1c * (jnp.concatenate([attn_c, rw_c], axis=-1) @ w_out)
        xc = xc + g2c * swiglu(modulate(rmsnorm(xc, norm2_g), sh2c, sc2c), ffn_w1, ffn_w3, ffn_w2)
    return x, xc


def setup_inputs(seed: int = 0) -> dict:
    key = jax.random.key(seed)
    ks = jax.random.split(key, 27)
    nrm = lambda k, shape, s: jax.random.normal(k, shape, jnp.float32) * s
    uni = lambda k, shape, lo, hi: jax.random.uniform(k, shape, jnp.float32, lo, hi)
    L = DEPTH
    return {
        'x': nrm(ks[0], (BATCH, SEQ, D_MODEL), 1.0),
        'c': nrm(ks[1], (BATCH, D_MODEL), 1.0),
        'ctx': nrm(ks[2], (BATCH, CTX_LEN, D_MODEL), 1.0),
        'c_ctx': nrm(ks[3], (D_MODEL,), 1.0),
        'ada_w': nrm(ks[4], (L, D_MODEL, 6 * D_MODEL), 0.5 * D_MODEL ** -0.5),
        'ada_b': nrm(ks[5], (L, 6 * D_MODEL), 0.02),
        'norm1_g': 1.0 + nrm(ks[6], (L, D_MODEL), 0.05),
        'norm2_g': 1.0 + nrm(ks[7], (L, D_MODEL), 0.05),
        'w_in': nrm(ks[8], (L, D_MODEL, D_IN), D_MODEL ** -0.5),
        'attn_sink': nrm(ks[9], (L, N_HEADS_A), 0.5),
        'ts_prev': uni(ks[10], (L, D_IN_RWKV), 0.0, 0.5),
        'ts_next': uni(ks[11], (L, D_IN_RWKV), 0.0, 0.5),
        'w0': uni(ks[12], (L, 2, D_RWKV), -4.0, 1.0),
        'w_up': nrm(ks[13], (L, 2, D_DECAY_LORA, D_RWKV), 0.1),
        'a0': nrm(ks[14], (L, 2, D_RWKV), 0.5),
        'a_up': nrm(ks[15], (L, 2, D_AAA_LORA, D_RWKV), 0.1),
        'k_k': 0.85 + nrm(ks[16], (L, D_RWKV), 0.05),
        'k_a': 1.0 + nrm(ks[17], (L, D_RWKV), 0.05),
        'r_k': nrm(ks[18], (L, D_RWKV), 0.1),
        'g_up': nrm(ks[19], (L, D_GATE_LORA, D_RWKV), D_GATE_LORA ** -0.5),
        'lnx_g': 1.0 + nrm(ks[20], (L, D_RWKV), 0.05),
        'lnx_b': nrm(ks[21], (L, D_RWKV), 0.02),
        'w_out': nrm(ks[22], (L, D_MIX, D_MODEL), D_MIX ** -0.5),
        'ffn_w1': nrm(ks[23], (L, D_MODEL, D_FF), D_MODEL ** -0.5),
        'ffn_w3': nrm(ks[24], (L, D_MODEL, D_FF), D_MODEL ** -0.5),
        'ffn_w2': nrm(ks[25], (L, D_FF, D_MODEL), D_FF ** -0.5),
        'final_norm_g': 1.0 + nrm(ks[26], (D_MODEL,), 0.05),
    }


def reference(x, c, ctx, c_ctx, ada_w, ada_b, norm1_g, norm2_g, w_in, attn_sink, ts_prev, ts_next,
              w0, w_up, a0, a_up, k_k, k_a, r_k, g_up, lnx_g, lnx_b, w_out, ffn_w1, ffn_w3, ffn_w2,
              final_norm_g):
    seq_len = x.shape[1]
    rows = seq_len // GRID_W
    row = jnp.repeat(jnp.arange(rows, dtype=jnp.int32), GRID_W)
    col = jnp.tile(jnp.arange(GRID_W, dtype=jnp.int32), rows)
    xc = ctx
    for i in range(DEPTH):
        x, xc = hybrid_layer(x, xc, c, c_ctx, row, col, ada_w[i], ada_b[i], norm1_g[i], norm2_g[i],
                             w_in[i], attn_sink[i], ts_prev[i], ts_next[i], w0[i], w_up[i], a0[i],
                             a_up[i], k_k[i], k_a[i], r_k[i], g_up[i], lnx_g[i], lnx_b[i], w_out[i],
                             ffn_w1[i], ffn_w3[i], ffn_w2[i], update_ctx=(i < DEPTH - 1))
    return rmsnorm(x, final_norm_g)
```

```python
import numpy as np
import concourse.bass as bass
import concourse.mybir as mybir
from concourse.bass_utils import run_bass_kernel_spmd

F32 = mybir.dt.float32
BF16 = mybir.dt.bfloat16
ALU = mybir.AluOpType
ACTF = mybir.ActivationFunctionType
AX = mybir.AxisListType

D = 2048
SEQ = 8192
CTX = 256
TT = SEQ + CTX
KC = D // 128
NA = 704
NR = 1184
NCOL = NA + NR
DFF = 5632
CH = 64
NCH = TT // CH
TOK = 2048


class Region:
    __slots__ = ("name", "w", "r")

    def __init__(self, name):
        self.name = name
        self.w = {}
        self.r = {}


class Emitter:
    def __init__(self, nc):
        self.nc = nc
        self.eng = {}
        self.ops = {}
        self.sems = {}
        self.semtot = {}
        for name, obj in (("pe", nc.tensor), ("dve", nc.vector), ("act", nc.scalar),
                          ("pool", nc.gpsimd), ("sp", nc.sync)):
            self.eng[name] = obj
            self.ops[name] = []
            self.sems[name] = nc.alloc_semaphore("e_" + name)
            self.semtot[name] = 0
        self.seen = {n: {} for n in self.eng}
        self.nreg = 0

    def region(self, name=None):
        self.nreg += 1
        return Region(name or "r%d" % self.nreg)

    def dsem(self, name):
        key = "d_" + name
        self.sems[key] = self.nc.alloc_semaphore(key)
        self.semtot[key] = 0
        return key

    def _deps(self, reads, writes):
        deps = {}
        for r in reads:
            for k, v in r.w.items():
                deps[k] = max(deps.get(k, 0), v)
        for w in writes:
            for k, v in w.w.items():
                deps[k] = max(deps.get(k, 0), v)
            for k, v in w.r.items():
                deps[k] = max(deps.get(k, 0), v)
        return deps

    def _emit_waits(self, e, deps):
        seen = self.seen[e]
        for k, v in deps.items():
            if k == e and e == "pe":
                continue
            if k.startswith("d_"):
                v = self.semtot[k]
            if seen.get(k, 0) >= v:
                continue
            seen[k] = v
            sem = self.sems[k]
            self.ops[e].append(lambda eng, sem=sem, v=v: eng.wait_ge(sem, v))

    def _record(self, reads, writes, key, val):
        for r in reads:
            r.r[key] = max(r.r.get(key, 0), val)
        for w in writes:
            w.w = {key: val}
            w.r = {}

    def op(self, e, fn, reads=(), writes=()):
        deps = self._deps(reads, writes)
        self._emit_waits(e, deps)
        self.semtot[e] += 1
        val = self.semtot[e]
        sem = self.sems[e]
        self.ops[e].append(lambda eng, fn=fn, sem=sem: fn(eng).then_inc(sem, 1))
        self._record(reads, writes, e, val)

    def dma(self, q, semkey, out, in_, reads=(), writes=(), **kw):
        deps = self._deps(reads, writes)
        self._emit_waits(q, deps)
        self.semtot[semkey] += 16
        val = self.semtot[semkey]
        sem = self.sems[semkey]
        self.ops[q].append(lambda eng, sem=sem, out=out, in_=in_, kw=kw:
                           eng.dma_start(out=out, in_=(in_() if callable(in_) else in_), **kw).then_inc(sem, 16))
        self._record(reads, writes, semkey, val)

    def raw(self, e, fn):
        self.ops[e].append(fn)

    def wait_all(self, e, regions):
        deps = {}
        for r in regions:
            for k, v in list(r.w.items()) + list(r.r.items()):
                deps[k] = max(deps.get(k, 0), v)
        self._emit_waits(e, deps)

    def run(self, block):
        ops = self.ops

        @block.tensor
        def _(eng):
            for f in ops["pe"]:
                f(eng)

        @block.vector
        def _(eng):
            for f in ops["dve"]:
                f(eng)

        @block.scalar
        def _(eng):
            for f in ops["act"]:
                f(eng)

        @block.gpsimd
        def _(eng):
            for f in ops["pool"]:
                f(eng)

        @block.sync
        def _(eng):
            for f in ops["sp"]:
                f(eng)


class Rot:
    def __init__(self, items):
        self.items = items
        self.i = 0

    def next(self):
        it = self.items[self.i % len(self.items)]
        self.i += 1
        return it


DBG = {}
CNEG = -0.6065306597126334
NEG = -1.0e30
C_ID = 0
C_MP = 128
C_MN = 256
C_MF = 384
C_MBK = 512
C_ML = 1024
C_RS = 1152
NCONST = 1664
P_SINK = 0
P_TS = 4
P_HP = 40
P_N2 = 68
P_FG = 84
NPP = 100


def build_program(stop_after=99, debug=False):
    from contextlib import ExitStack
    nc = bass.Bass("TRN2", target_bir_lowering=False)
    em = Emitter(nc)
    R = em.region

    def din(name, shape, dt=F32):
        return nc.dram_tensor(name, list(shape), dt, kind="ExternalInput")

    def dscr(name, shape, dt=F32):
        if debug:
            return nc.dram_tensor(name, list(shape), dt, kind="ExternalOutput")
        return nc.dram_tensor(name, list(shape), dt)

    def I(eng, method, r, w, *args, **kw):
        em.op(eng, lambda e: getattr(e, method)(*args, **kw), r, w)

    xT = din("xT", [D, TT])
    xtok = din("xtok", [D, TOK])
    c_col = din("c_col", [128, KC, 2])
    ada_w = din("ada_w", [D, 6 * D])
    ada_bT = din("ada_bT", [128, 96])
    n1g = din("n1g", [128, KC])
    Wc = din("Wc", [D, NCOL])
    consts = din("consts", [128, NCONST])
    pp = din("pp", [128, NPP])
    rope = din("rope", [64, 2, SEQ])
    rowb = din("rowb", [64, 4, 2, 64])
    lora = din("lora", [64, 2, 2, 4, 64])
    gup = din("gup", [160, 256])
    w_out = din("w_out", [D, D])
    w1 = din("w1", [D, DFF])
    w3 = din("w3", [D, DFF])
    w2 = din("w2", [DFF, D])
    outT = nc.dram_tensor("outT", [D, TOK], F32, kind="ExternalOutput")
    zA = dscr("zA", [NA, TT], BF16)
    zR = dscr("zR", [NR, TT], F32)
    PA = [nc.dram_tensor("PA%d" % h, [NCH, 64, 256], F32) for h in range(4)]
    QB = [nc.dram_tensor("QB%d" % h, [NCH, 128, 128], F32) for h in range(4)]
    BG = [nc.dram_tensor("BG%d" % h, [NCH, 64, 128], F32) for h in range(4)]
    mixL = nc.dram_tensor("mixL", [4 * 512, TOK], BF16)
    mixG = nc.dram_tensor("mixG", [4 * 4 * 512, TOK], BF16)
    r_zA, r_zR, r_mixL, r_mixG = R("zA"), R("zR"), R("mixL"), R("mixG")
    r_PA = [R() for _ in range(4)]
    r_QB = [R() for _ in range(4)]
    r_BG = [R() for _ in range(4)]

    s_dbg = em.dsem("dbg")
    r_dbg = R("dbg")

    def dump(name, ap, shape, dt, reads):
        if not debug:
            return
        t = nc.dram_tensor("dbg_" + name, list(shape), dt, kind="ExternalOutput")
        em.dma("sp", s_dbg, t.ap(), ap, reads=reads, writes=[r_dbg])

    ALLE = ("sp", "pe", "act", "dve", "pool")

    def barrier(regions):
        for e_ in ALLE:
            em.wait_all(e_, regions)

    with ExitStack() as top:
        sb = lambda name, shape, dt=F32: top.enter_context(nc.sbuf_tensor(name, list(shape), dt))
        modT = sb("modT", [128, 96, 2])
        r_modT = R("modT")
        A1 = sb("A1", [128, KC, 2])
        r_A1 = R("A1")
        A2 = sb("A2", [128, KC])
        r_A2 = R("A2")
        ones_bf = sb("ones_bf", [128, 128], BF16)
        r_ones = R("ones")
        I("pool", "memset", [], [r_ones], ones_bf[:], 1.0)
        cst = sb("cst", [128, NCONST])
        ppt = sb("ppt", [128, NPP])
        r_cst = R("cst")
        s_cst = em.dsem("cst")
        em.dma("sp", s_cst, cst[:], consts[:, :], writes=[r_cst])
        em.dma("sp", s_cst, ppt[:], pp[:, :], writes=[r_cst])
        ident_bf = sb("ident_bf", [128, 128], BF16)
        I("dve", "tensor_copy", [r_cst], [r_ones], out=ident_bf[:], in_=cst[:, C_ID:C_ID + 128])
        ident_f = cst[:, C_ID:C_ID + 128]

        with ExitStack() as ph:
            psb = lambda name, shape, dt=F32: ph.enter_context(nc.sbuf_tensor(name, list(shape), dt))
            csb = psb("csb", [128, KC, 2])
            r_csb = R()
            s_c = em.dsem("c")
            em.dma("sp", s_c, csb[:], c_col[:, :, :], writes=[r_csb])
            sc = psb("sc", [128, KC, 2])
            r_sc = R()
            abT = psb("abT", [128, 96])
            em.dma("sp", s_c, abT[:], ada_bT[:, :], writes=[r_csb])
            g1n = psb("g1n", [128, KC])
            em.dma("sp", s_c, g1n[:], n1g[:, :], writes=[r_csb])
            I("act", "activation", [r_csb], [r_sc], out=sc[:], in_=csb[:], func=ACTF.Silu)
            blks = [(psb("adablk%d" % i, [128, KC, 512]), R(), em.dsem("ada%d" % i)) for i in range(2)]
            rot = Rot(blks)
            ps_mod = ph.enter_context(nc.psum_tensor("ps_mod", [128, 96, 2], F32))
            r_psmod = R()
            ada_v = ada_w.ap().rearrange("(kc p) n -> p kc n", p=128)
            for cb in range(DBG.get('p0', 24)):
                t, rg, sm = rot.next()
                for half in range(2):
                    em.dma("sp", sm, t[:, half * 8:(half + 1) * 8, :],
                           ada_v[:, half * 8:(half + 1) * 8, cb * 512:(cb + 1) * 512], writes=[rg])
                for j4 in range(4):
                    j = cb * 4 + j4
                    for kc in range(KC):
                        I("pe", "matmul", [rg, r_sc], [r_psmod], ps_mod[:, j, :],
                          t[:, kc, j4 * 128:(j4 + 1) * 128], sc[:, kc, :], start=(kc == 0), stop=(kc == KC - 1))
            I("dve", "tensor_tensor", [r_psmod, r_csb], [r_modT], out=modT[:], in0=ps_mod[:],
              in1=abT[:].unsqueeze(2).broadcast_to([128, 96, 2]), op=ALU.add)
            I("dve", "tensor_scalar", [r_modT], [r_A1], out=A1[:], in0=modT[:, 16:32, :], scalar1=1.0,
              scalar2=None, op0=ALU.add)
            I("dve", "tensor_tensor", [r_A1, r_csb], [r_A1], out=A1[:], in0=A1[:],
              in1=g1n[:].unsqueeze(2).broadcast_to([128, KC, 2]), op=ALU.mult)
            I("dve", "tensor_scalar", [r_modT], [r_A2], out=A2[:], in0=modT[:, 64:80, 0], scalar1=1.0,
              scalar2=None, op0=ALU.add)
            I("dve", "tensor_tensor", [r_A2, r_cst], [r_A2], out=A2[:], in0=A2[:], in1=ppt[:, P_N2:P_N2 + 16],
              op=ALU.mult)
            barrier([r_A1, r_A2, r_modT, r_psmod, r_sc, r_csb] + [b[1] for b in blks])

        with ExitStack() as ph:
            psb = lambda name, shape, dt=F32: ph.enter_context(nc.sbuf_tensor(name, list(shape), dt))
            Wb = psb("Wb", [128, KC, NCOL], BF16)
            r_Wb = R()
            Wc_v = Wc.ap().rearrange("(kc p) n -> p kc n", p=128)
            wst = [(psb("wst%d" % i, [128, NCOL]), R(), em.dsem("wst%d" % i)) for i in range(2)]
            wrot = Rot(wst)
            for kc in range(KC):
                w_t, r_w, s_wst = wrot.next()
                em.dma("sp", s_wst, w_t[:], Wc_v[:, kc, :], writes=[r_w])
                I("pool", "tensor_copy", [r_w], [r_Wb], out=Wb[:, kc, :], in_=w_t[:])
            xs = [(psb("xt%d" % i, [128, KC, 512]), R(), em.dsem("xt%d" % i)) for i in range(2)]
            xrot = Rot(xs)
            sq = psb("sq", [128, KC, 512], BF16)
            r_sq = R()
            hT = psb("hT", [128, KC, 512], BF16)
            r_hT = R()
            xn, r_xn = sq, r_sq
            rstd = psb("rstd", [128, 512])
            r_rstd = R()
            stA = [(psb("stA%d" % i, [128, 512], BF16), R(), em.dsem("stA%d" % i)) for i in range(3)]
            stR = [(psb("stR%d" % i, [128, 512], F32), R(), em.dsem("stR%d" % i)) for i in range(3)]
            rotA, rotR = Rot(stA), Rot(stR)
            ps_ss = ph.enter_context(nc.psum_tensor("ps_ss", [128, 512], F32))
            r_psss = R()
            pss = [(ph.enter_context(nc.psum_tensor("ps_z%d" % i, [128, 512], F32)), R()) for i in range(4)]
            psrot = Rot(pss)
            xT_v = xT.ap().rearrange("(kc p) t -> p kc t", p=128)
            tiles = [(0, 256, 1)] + [(256 + i * 512, 512, 0) for i in range(16)]
            ctiles = [(i * 128, 128) for i in range(5)] + [(640, 64)] + \
                     [(NA + i * 128, 128) for i in range(9)] + [(NA + 1152, 32)]
            evac_i = 0
            for (t0, N, s) in tiles[:DBG.get('p1', 17)]:
                xt, r_xt, s_xt = xrot.next()
                for half in range(2):
                    em.dma("sp", s_xt, xt[:, half * 8:(half + 1) * 8, 0:N],
                           xT_v[:, half * 8:(half + 1) * 8, t0:t0 + N], writes=[r_xt])
                I("act", "activation", [r_xt], [r_sq], out=sq[:, :, 0:N], in_=xt[:, :, 0:N], func=ACTF.Square)
                for kc in range(KC):
                    I("pe", "matmul", [r_sq, r_ones], [r_psss], ps_ss[:, 0:N], ones_bf[:], sq[:, kc, 0:N],
                      start=(kc == 0), stop=(kc == KC - 1))
                I("dve", "tensor_scalar", [r_psss], [r_rstd], out=rstd[:, 0:N], in0=ps_ss[:, 0:N],
                  scalar1=1.0 / D, scalar2=1e-6, op0=ALU.mult, op1=ALU.add)
                I("act", "activation", [r_rstd], [r_rstd], out=rstd[:, 0:N], in_=rstd[:, 0:N], func=ACTF.Sqrt)
                I("dve", "reciprocal", [r_rstd], [r_rstd], out=rstd[:, 0:N], in_=rstd[:, 0:N])
                I("dve", "tensor_tensor", [r_xt, r_rstd], [r_xn], out=xn[:, :, 0:N], in0=xt[:, :, 0:N],
                  in1=rstd[:, 0:N].unsqueeze(1).broadcast_to([128, KC, N]), op=ALU.mult)
                for kc in range(KC):
                    if kc % 2 == 0:
                        I("act", "activation", [r_xn, r_A1, r_modT], [r_hT], out=hT[:, kc, 0:N], in_=xn[:, kc, 0:N],
                          func=ACTF.Identity, scale=A1[:, kc, s:s + 1], bias=modT[:, kc, s:s + 1])
                    else:
                        I("pool", "tensor_scalar", [r_xn, r_A1, r_modT], [r_hT], out=hT[:, kc, 0:N],
                          in0=xn[:, kc, 0:N], scalar1=A1[:, kc, s:s + 1], scalar2=modT[:, kc, s:s + 1],
                          op0=ALU.mult, op1=ALU.add)
                for (c0, M) in ctiles:
                    ps, r_ps = psrot.next()
                    for kc in range(KC):
                        I("pe", "matmul", [r_Wb, r_hT], [r_ps], ps[0:M, 0:N], Wb[:, kc, c0:c0 + M], hT[:, kc, 0:N],
                          start=(kc == 0), stop=(kc == KC - 1))
                    isA = c0 < NA
                    st, r_st, s_st = (rotA if isA else rotR).next()
                    if evac_i % 2 == 0:
                        I("act", "activation", [r_ps], [r_st], out=st[0:M, 0:N], in_=ps[0:M, 0:N], func=ACTF.Copy)
                    else:
                        I("dve", "tensor_copy", [r_ps], [r_st], out=st[0:M, 0:N], in_=ps[0:M, 0:N])
                    evac_i += 1
                    if isA:
                        em.dma("sp", s_st, zA[c0:c0 + M, t0:t0 + N], st[0:M, 0:N], reads=[r_st], writes=[r_zA])
                    else:
                        em.dma("sp", s_st, zR[c0 - NA:c0 - NA + M, t0:t0 + N], st[0:M, 0:N],
                               reads=[r_st], writes=[r_zR])
            barrier([r_zA, r_zR, r_hT, r_xn, r_sq, r_Wb] + [x[1] for x in stA + stR + pss + xs + wst])

        if stop_after >= 2:
          with ExitStack() as ph:
            psb = lambda name, shape, dt=F32: ph.enter_context(nc.sbuf_tensor(name, list(shape), dt))
            pst = lambda name, shape, dt=F32: ph.enter_context(nc.psum_tensor(name, list(shape), dt))
            KT = psb("KT", [64, TT + 128], BF16)
            r_KT = R()
            Vtm = psb("Vtm", [128, 67, 64], BF16)
            r_Vtm = R()
            QT = psb("QT", [64, 4, SEQ], BF16)
            r_QT = R()
            vT = psb("vT", [64, TT], BF16)
            r_vT = R()
            I("pool", "memset", [], [r_KT], KT[:, TT:TT + 128], 0.0)
            I("pool", "memset", [], [r_Vtm], Vtm[:, 66, :], 0.0)
            s_a0 = em.dsem("a0")
            if not DBG.get("skipA"):
                em.dma("sp", s_a0, KT[:, 0:CTX], zA[512:576, 0:CTX], reads=[r_zA], writes=[r_KT])
                em.dma("sp", s_a0, vT[:], zA[640:704, :], reads=[r_zA], writes=[r_vT])
            psV = pst("psV", [128, 8, 64], BF16)
            r_psV = R()
            for b0 in range(0, DBG.get('p2v', 66), 8):
                nb = min(8, 66 - b0)
                for j in range(nb):
                    I("pe", "transpose", [r_vT, r_ones], [r_psV], psV[:, j, :], vT[:, (b0 + j) * 128:(b0 + j + 1) * 128],
                      ident_bf[0:64, 0:64])
                I("dve", "tensor_copy", [r_psV], [r_Vtm], out=Vtm[:, b0:b0 + nb, :], in_=psV[:, 0:nb, :])
            cs = [(psb("cs%d" % i, [64, 2, 512]), R(), em.dsem("cs%d" % i)) for i in range(2)]
            csrot = Rot(cs)
            zz = [(psb("zz%d" % i, [64, 2, 512], BF16), R(), em.dsem("zz%d" % i)) for i in range(3)]
            zrot = Rot(zz)
            t1 = psb("t1", [64, 512])
            t2 = psb("t2", [64, 512])
            r_t1, r_t2 = R(), R()
            for i in range(DBG.get('p2r', 16)):
                c_t, r_c, s_cc = csrot.next()
                em.dma("sp", s_cc, c_t[:], rope[:, :, i * 512:(i + 1) * 512], writes=[r_c])
                for src in range(5):
                    z_t, r_z, s_z = zrot.next()
                    row = src * 64 if src < 4 else 512
                    prow = 256 + src * 64 if src < 4 else 576
                    em.dma("sp", s_z, z_t[:, 0, :], zA[row:row + 64, CTX + i * 512:CTX + (i + 1) * 512],
                           reads=[r_zA], writes=[r_z])
                    em.dma("sp", s_z, z_t[:, 1, :], zA[prow:prow + 64, CTX + i * 512:CTX + (i + 1) * 512],
                           reads=[r_zA], writes=[r_z])
                    I("dve", "tensor_tensor", [r_z, r_c], [r_t1], out=t1[:], in0=z_t[:, 0, :], in1=c_t[:, 0, :],
                      op=ALU.mult)
                    I("pool", "tensor_tensor", [r_z, r_c], [r_t2], out=t2[:], in0=z_t[:, 1, :], in1=c_t[:, 1, :],
                      op=ALU.mult)
                    if src < 4:
                        I("dve", "tensor_tensor", [r_t1, r_t2], [r_QT], out=QT[:, src, i * 512:(i + 1) * 512],
                          in0=t1[:], in1=t2[:], op=ALU.add)
                    else:
                        I("dve", "tensor_tensor", [r_t1, r_t2], [r_KT],
                          out=KT[:, CTX + i * 512:CTX + (i + 1) * 512], in0=t1[:], in1=t2[:], op=ALU.add)
            sA = [(pst("sA%d" % i, [128, 512]), R()) for i in range(2)]
            sB = [(pst("sB%d" % i, [128, 128]), R()) for i in range(2)]
            pTa = pst("pTa", [128, 4, 128])
            pTb = pst("pTb", [128, 128])
            r_pTa, r_pTb = R(), R()
            psO = pst("psO", [64, 128])
            r_psO = R()
            pr = [(psb("pA%d" % i, [128, 640], BF16), R()) for i in range(2)]
            pT_sb = [(psb("pTs%d" % i, [128, 5, 128], BF16), R()) for i in range(2)]
            sm = [(psb("sm%d" % i, [128, 8]), R()) for i in range(2)]
            dg = [(psb("dg%d" % i, [128, 128], BF16), R()) for i in range(2)]
            ost = [(psb("ost%d" % i, [64, 512], BF16), R(), em.dsem("ost%d" % i)) for i in range(2)]
            def attn_gen(hh, blk, par):
                o_t, r_o, s_o = ost[(hh * 16 + blk // 4) % 2]
                sa, r_sa = sA[par]
                sb_, r_sb = sB[par]
                p_t, r_p = pr[par]
                pt_t, r_pt = pT_sb[par]
                m_t, r_m = sm[par]
                d_t, r_d = dg[par]
                q_ap = QT[:, hh, blk * 128:(blk + 1) * 128]
                kc0 = CTX + (blk - 1) * 128
                I("pe", "matmul", [r_QT, r_KT], [r_sa], sa[:, 0:256], q_ap, KT[:, 0:256], start=True, stop=True)
                I("pe", "matmul", [r_QT, r_KT], [r_sa], sa[:, 256:512], q_ap, KT[:, kc0:kc0 + 256],
                  start=True, stop=True)
                I("pe", "matmul", [r_QT, r_KT], [r_sb], sb_[:, :], q_ap, KT[:, kc0 + 256:kc0 + 384],
                  start=True, stop=True)
                yield
                mp_c = C_MF if blk == 0 else C_MP
                mn_c = C_MF if blk == 63 else C_MN
                I("dve", "tensor_tensor", [r_sa, r_cst], [r_sa], out=sa[:, 256:384], in0=sa[:, 256:384],
                  in1=cst[:, mp_c:mp_c + 128], op=ALU.add)
                I("dve", "tensor_tensor", [r_sb, r_cst], [r_sb], out=sb_[:, :], in0=sb_[:, :],
                  in1=cst[:, mn_c:mn_c + 128], op=ALU.add)
                I("dve", "reduce_max", [r_sa], [r_m], out=m_t[:, 0:1], in_=sa[:, :], axis=AX.X)
                I("dve", "reduce_max", [r_sb], [r_m], out=m_t[:, 1:2], in_=sb_[:, :], axis=AX.X)
                I("dve", "tensor_tensor", [r_m], [r_m], out=m_t[:, 2:3], in0=m_t[:, 0:1], in1=m_t[:, 1:2],
                  op=ALU.max)
                I("dve", "tensor_scalar", [r_m, r_cst], [r_m], out=m_t[:, 3:4], in0=m_t[:, 2:3], scalar1=0.125,
                  scalar2=ppt[:, P_SINK + hh:P_SINK + hh + 1], op0=ALU.mult, op1=ALU.max)
                I("dve", "tensor_scalar", [r_m], [r_m], out=m_t[:, 4:5], in0=m_t[:, 3:4], scalar1=-1.0,
                  scalar2=None, op0=ALU.mult)
                yield
                I("act", "activation", [r_sa, r_m], [r_p, r_m], out=p_t[:, 0:512], in_=sa[:, :], func=ACTF.Exp,
                  scale=0.125, bias=m_t[:, 4:5], accum_out=m_t[:, 5:6])
                I("act", "activation", [r_sb, r_m], [r_p, r_m], out=p_t[:, 512:640], in_=sb_[:, :], func=ACTF.Exp,
                  scale=0.125, bias=m_t[:, 4:5], accum_out=m_t[:, 6:7])
                I("act", "activation", [r_m, r_cst], [r_m], out=m_t[:, 7:8], in_=m_t[:, 4:5], func=ACTF.Exp,
                  scale=1.0, bias=ppt[:, P_SINK + hh:P_SINK + hh + 1])
                yield
                I("dve", "tensor_tensor", [r_m], [r_m], out=m_t[:, 5:6], in0=m_t[:, 5:6], in1=m_t[:, 6:7],
                  op=ALU.add)
                I("dve", "tensor_tensor", [r_m], [r_m], out=m_t[:, 5:6], in0=m_t[:, 5:6], in1=m_t[:, 7:8],
                  op=ALU.add)
                I("dve", "reciprocal", [r_m], [r_m], out=m_t[:, 6:7], in_=m_t[:, 5:6])
                I("dve", "tensor_scalar", [r_m, r_cst], [r_d], out=d_t[:], in0=cst[:, C_ID:C_ID + 128],
                  scalar1=m_t[:, 6:7], scalar2=None, op0=ALU.mult)
                yield
                for kb in range(4):
                    I("pe", "matmul", [r_p, r_d], [r_pTa], pTa[:, kb, :], p_t[:, kb * 128:(kb + 1) * 128], d_t[:],
                      start=True, stop=True)
                I("pe", "matmul", [r_p, r_d], [r_pTb], pTb[:, :], p_t[:, 512:640], d_t[:], start=True, stop=True)
                I("act", "activation", [r_pTa], [r_pt], out=pt_t[:, 0:4, :], in_=pTa[:, :, :], func=ACTF.Copy)
                I("dve", "tensor_copy", [r_pTb], [r_pt], out=pt_t[:, 4, :], in_=pTb[:, :])
                yield
                vblk = [0, 1, 2 + blk - 1 if blk > 0 else 0, 2 + blk, 2 + blk + 1]
                for kb in range(5):
                    I("pe", "matmul", [r_Vtm, r_pt], [r_psO], psO[:, :], Vtm[:, vblk[kb], :], pt_t[:, kb, :],
                      start=(kb == 0), stop=(kb == 4))
                I("dve", "tensor_copy", [r_psO], [r_o], out=o_t[:, (blk % 4) * 128:(blk % 4 + 1) * 128],
                  in_=psO[:, :])
                if blk % 4 == 3:
                    tok0 = (blk // 4) * 512
                    qd, col = tok0 // TOK, tok0 % TOK
                    em.dma("sp", s_o, mixL[qd * 512 + hh * 64:qd * 512 + hh * 64 + 64, col:col + 512], o_t[:],
                           reads=[r_o], writes=[r_mixL])
                yield

            for hh in range(DBG.get('p2h', 4)):
                for blk0 in range(0, 64, 2):
                    live = [attn_gen(hh, blk0 + j, j) for j in range(2)]
                    while live:
                        for g_ in list(live):
                            try:
                                next(g_)
                            except StopIteration:
                                live.remove(g_)
            barrier([r_mixL, r_KT, r_QT, r_Vtm, r_vT, r_psV, r_psO, r_pTa, r_pTb, r_t1, r_t2] +
                    [x[1] for x in sA + sB + pr + pT_sb + sm + dg + ost + cs + zz])

        if stop_after >= 3:
          with ExitStack() as ph:
            psb = lambda name, shape, dt=F32: ph.enter_context(nc.sbuf_tensor(name, list(shape), dt))
            pst = lambda name, shape, dt=F32: ph.enter_context(nc.psum_tensor(name, list(shape), dt))
            lo32 = psb("lo32", [64, 2, 2, 4, 64])
            lob = psb("lob", [64, 2, 2, 4, 64], BF16)
            gu32 = psb("gu32", [128, 2, 256])
            gub = psb("gub", [128, 2, 256], BF16)
            r_par = R()
            s_par = em.dsem("par")
            em.dma("sp", s_par, lo32[:], lora[:, :, :, :, :], writes=[r_par])
            em.dma("sp", s_par, gu32[:, 0, :], gup[0:128, :], writes=[r_par])
            em.dma("sp", s_par, gu32[0:32, 1, :], gup[128:160, :], writes=[r_par])
            r_lob = R()
            I("pool", "memset", [], [r_lob], gub[:, 1, :], 0.0)
            I("dve", "tensor_copy", [r_par], [r_lob], out=lob[:], in_=lo32[:])
            I("dve", "tensor_copy", [r_par], [r_lob], out=gub[:, 0, :], in_=gu32[:, 0, :])
            I("dve", "tensor_copy", [r_par], [r_lob], out=gub[0:32, 1, :], in_=gu32[0:32, 1, :])
            tsc = psb("tsc", [128, 18])
            I("dve", "tensor_tensor", [r_cst], [r_lob], out=tsc[:], in0=ppt[:, P_TS:P_TS + 18],
              in1=ppt[:, P_TS + 18:P_TS + 36], op=ALU.add)
            I("dve", "tensor_scalar", [r_lob], [r_lob], out=tsc[:], in0=tsc[:], scalar1=-1.0, scalar2=1.0,
              op0=ALU.mult, op1=ALU.add)
            omka = psb("omka", [64, 4])
            I("dve", "tensor_scalar", [r_cst], [r_lob], out=omka[:], in0=ppt[0:64, P_HP + 20:P_HP + 24], scalar1=-1.0,
              scalar2=1.0, op0=ALU.mult, op1=ALU.add)
            hp = lambda h, i: ppt[0:64, P_HP + i * 4 + h:P_HP + i * 4 + h + 1]
            ones64 = ones_bf[0:64, 0:64]
            BKbd = [(psb("BKbd%d" % i, [128, 2, 128], BF16), R()) for i in range(2)]
            for t_, r_ in BKbd:
                I("pool", "memset", [], [r_], t_[:], 0.0)
            zl = [(psb("zl%d" % i, [128, 514]), R(), em.dsem("zl%d" % i)) for i in range(3)]
            zlrot = Rot(zl)
            twd = psb("twd", [64, 2, 512], BF16)
            adz = psb("adz", [64, 2, 512], BF16)
            sgb = psb("sgb", [128, 2, 512], BF16)
            r_twd, r_adz, r_sgb = R(), R(), R()
            zsh = psb("zsh", [128, 512])
            r_zsh = R()
            I("pool", "memset", [], [r_sgb], sgb[:, 1, :], 0.0)
            idpad = psb("idpad", [64, 128], BF16)
            I("pool", "memset", [], [r_lob], idpad[:], 0.0)
            I("dve", "tensor_copy", [r_cst, r_lob], [r_lob], out=idpad[:, 0:64], in_=cst[0:64, C_ID:C_ID + 64])
            rr = psb("rr", [64, 512]); kk_ = psb("kk_", [64, 512]); vv = psb("vv", [64, 512])
            r_rr, r_kk, r_vv = R(), R(), R()
            sg = psb("sg", [64, 2, 512]); av = psb("av", [64, 2, 512]); kd = psb("kd", [64, 2, 512])
            bv = psb("bv", [64, 2, 512]); Pc = psb("Pc", [64, 2, 512])
            r_sg, r_av, r_kd, r_bv, r_Pc = R(), R(), R(), R(), R()
            kkn = psb("kkn", [64, 512]); nkk = psb("nkk", [64, 512]); tmpa = psb("tmpa", [64, 512])
            tmpb = psb("tmpb", [64, 512]); kq = psb("kq", [64, 512], BF16)
            r_kkn, r_nkk, r_tmpa, r_tmpb, r_kq = R(), R(), R(), R(), R()
            Ein = psb("Ein", [64, 2, 512]); Eng = psb("Eng", [64, 2, 512]); Eex = psb("Eex", [64, 2, 512])
            EC = psb("EC", [64, 2, 512])
            r_Ein, r_Eng, r_Eex, r_EC = R(), R(), R(), R()
            opsets = []
            for i_ in range(2):
                opsets.append((psb("AR%d" % i_, [64, 8, 2, 2, 64], BF16), psb("Bt%d" % i_, [64, 8, 2, 64], BF16),
                               psb("Kt%d" % i_, [64, 8, 2, 64], BF16), psb("Bh%d" % i_, [64, 8, 2, 64], BF16),
                               psb("Kh%d" % i_, [64, 8, 2, 64], BF16), psb("Vd%d" % i_, [64, 8, 2, 64], BF16),
                               psb("RK%d" % i_, [64, 8, 64], BF16), psb("gam%d" % i_, [64, 2, 8]),
                               R(), R(), R(), R(), R(), R(), R(), R()))
            REC = [None]
            psX = [(pst("psX%d" % i, [64, 512]), R()) for i in range(2)]
            pxrot = Rot(psX)
            psT = pst("psT", [128, 4, 64], BF16); r_psT = R()
            psG = pst("psG", [128, 512]); r_psG = R()
            psL = pst("psL", [128, 2, 128]); r_psL = R()
            psY = [(pst("psY%d" % i, [128, 128]), R()) for i in range(2)]
            ps45 = pst("ps45", [128, 512]); r_ps45 = R()
            Vtm2 = [(psb("Vtm2_%d" % i, [128, 128], BF16), R()) for i in range(2)]
            for t_, r_ in Vtm2:
                I("pool", "memset", [], [r_], t_[:], 0.0)
            GBK = [(psb("GBK%d" % i, [128, 512], BF16), R()) for i in range(2)]
            NL = [(psb("NL%d" % i, [128, 2, 128], BF16), R()) for i in range(6)]
            nlrots = [Rot(NL[0:3]), Rot(NL[3:6])]
            Yc = [(psb("Yc%d" % i, [128, 128], BF16), R()) for i in range(4)]
            ycrots = [Rot(Yc[0:2]), Rot(Yc[2:4])]
            Wsb = [(psb("Wsb%d" % i, [128, 128], BF16), R()) for i in range(2)]
            oPA = [(psb("oPA%d" % i, [64, 256]), R(), em.dsem("oPA%d" % i)) for i in range(2)]
            oQB = [(psb("oQB%d" % i, [128, 128]), R(), em.dsem("oQB%d" % i)) for i in range(2)]
            oBG = [(psb("oBG%d" % i, [64, 128]), R(), em.dsem("oBG%d" % i)) for i in range(2)]
            I0 = I

            def I(eng, method, r, w, *args, **kw):
                if REC[0] is not None:
                    REC[0].append(("I", (eng, method, r, w) + args, kw))
                else:
                    I0(eng, method, r, w, *args, **kw)

            def DM(q, semk, out, in_, reads=(), writes=()):
                if REC[0] is not None:
                    REC[0].append(("D", (q, semk, out, in_), dict(reads=reads, writes=writes)))
                else:
                    em.dma(q, semk, out, in_, reads=reads, writes=writes)

            def replay(ops):
                for kind, a, kw in ops:
                    if kind == "I":
                        I0(*a, **kw)
                    else:
                        em.dma(*a, **kw)

            unit_i = 0
            c3 = lambda ap, n: ap.rearrange("p (c t) -> p c t", t=64)

            def shifted_load(row0, nrows, grp, t0, N, first, last):
                z_t, r_z, s_z = zlrot.next()
                lo = 0 if first else 1
                hi = 0 if last else 1
                if first:
                    I("pool", "memset", [], [r_z], z_t[0:nrows, 0:1], 0.0)
                if last:
                    I("pool", "memset", [], [r_z], z_t[0:nrows, N + 1:N + 2], 0.0)
                DM("sp", s_z, z_t[0:nrows, 1 - lo:N + 1 + hi], zR[row0:row0 + nrows, t0 - lo:t0 + N + hi],
                  reads=[r_zR], writes=[r_z])
                I("dve", "tensor_scalar", [r_z, r_lob], [r_zsh], out=zsh[0:nrows, 0:N], in0=z_t[0:nrows, 1:N + 1],
                  scalar1=tsc[0:nrows, grp:grp + 1], scalar2=None, op0=ALU.mult)
                I("dve", "scalar_tensor_tensor", [r_z, r_cst, r_zsh], [r_zsh], out=zsh[0:nrows, 0:N],
                  in0=z_t[0:nrows, 0:N], scalar=ppt[0:nrows, P_TS + grp:P_TS + grp + 1], in1=zsh[0:nrows, 0:N],
                  op0=ALU.mult, op1=ALU.add)
                I("dve", "scalar_tensor_tensor", [r_z, r_cst, r_zsh], [r_zsh], out=zsh[0:nrows, 0:N],
                  in0=z_t[0:nrows, 2:N + 2], scalar=ppt[0:nrows, P_TS + 18 + grp:P_TS + 18 + grp + 1],
                  in1=zsh[0:nrows, 0:N], op0=ALU.mult, op1=ALU.add)

            tiles = [(0, 256)] + [(256 + i * 512, 512) for i in range(16)]
            for (t0, N) in tiles[:DBG.get('p3t', 17)]:
                first = t0 in (0, CTX)
                last = (t0 + N) in (CTX, TT)
                nc_ = N // 64
                ch0 = t0 // 64
                for d in range(2):
                    shifted_load(768 + d * 64, 64, 12 + d, t0, N, first, last)
                    I("act", "activation", [r_zsh], [r_twd], out=twd[:, d, 0:N], in_=zsh[0:64, 0:N], func=ACTF.Tanh)
                for d in range(2):
                    shifted_load(896 + d * 64, 64, 14 + d, t0, N, first, last)
                    I("act", "activation", [r_zsh], [r_adz], out=adz[:, d, 0:N], in_=zsh[0:64, 0:N], func=ACTF.Copy)
                shifted_load(1024, 128, 16, t0, N, first, last)
                I("act", "activation", [r_zsh], [r_sgb], out=sgb[:, 0, 0:N], in_=zsh[:, 0:N], func=ACTF.Sigmoid)
                shifted_load(1152, 32, 17, t0, N, first, last)
                I("act", "activation", [r_zsh], [r_sgb], out=sgb[0:32, 1, 0:N], in_=zsh[0:32, 0:N], func=ACTF.Sigmoid)
                def prep(h, par):
                    (AR, Bt, Kt, Bh, Kh, Vd, RK, gam, r_AR, r_Bt, r_Kt, r_Bh, r_Kh, r_Vd, r_RK, r_gam) = opsets[par]
                    shifted_load(h * 64, 64, h, t0, N, first, last)
                    I("act", "activation", [r_zsh], [r_rr], out=rr[:, 0:N], in_=zsh[0:64, 0:N], func=ACTF.Copy)
                    shifted_load(256 + h * 64, 64, 4 + h, t0, N, first, last)
                    I("act", "activation", [r_zsh], [r_kk], out=kk_[:, 0:N], in_=zsh[0:64, 0:N], func=ACTF.Copy)
                    shifted_load(512 + h * 64, 64, 8 + h, t0, N, first, last)
                    I("act", "activation", [r_zsh], [r_vv], out=vv[:, 0:N], in_=zsh[0:64, 0:N], func=ACTF.Copy)
                    for d in range(2):
                        px, r_px = pxrot.next()
                        I("pe", "matmul", [r_lob, r_twd], [r_px], px[:, 0:N], lob[:, 0, d, h, :], twd[:, d, 0:N],
                          start=True, stop=True)
                        I("act", "activation", [r_px, r_cst], [r_sg], out=sg[:, d, 0:N], in_=px[:, 0:N],
                          func=ACTF.Sigmoid, bias=hp(h, d), scale=1.0)
                        px, r_px = pxrot.next()
                        I("pe", "matmul", [r_lob, r_adz], [r_px], px[:, 0:N], lob[:, 1, d, h, :], adz[:, d, 0:N],
                          start=True, stop=True)
                        I("act", "activation", [r_px, r_cst], [r_av], out=av[:, d, 0:N], in_=px[:, 0:N],
                          func=ACTF.Sigmoid, bias=hp(h, 2 + d), scale=1.0)
                    I("dve", "tensor_scalar", [r_kk, r_cst], [r_kkn], out=kkn[:, 0:N], in0=kk_[:, 0:N], scalar1=hp(h, 4),
                      scalar2=None, op0=ALU.mult)
                    I("pool", "tensor_tensor", [r_kkn], [r_kq], out=kq[:, 0:N], in0=kkn[:, 0:N], in1=kkn[:, 0:N],
                      op=ALU.mult)
                    px, r_px = pxrot.next()
                    I("pe", "matmul", [r_ones, r_kq], [r_px], px[:, 0:N], ones64, kq[:, 0:N], start=True, stop=True)
                    I("dve", "tensor_scalar", [r_px], [r_tmpa], out=tmpa[:, 0:N], in0=px[:, 0:N], scalar1=1e-12,
                      scalar2=None, op0=ALU.max)
                    I("act", "activation", [r_tmpa], [r_tmpa], out=tmpa[:, 0:N], in_=tmpa[:, 0:N], func=ACTF.Sqrt)
                    I("dve", "reciprocal", [r_tmpa], [r_tmpa], out=tmpa[:, 0:N], in_=tmpa[:, 0:N])
                    I("dve", "tensor_tensor", [r_kkn, r_tmpa], [r_kkn], out=kkn[:, 0:N], in0=kkn[:, 0:N],
                      in1=tmpa[:, 0:N], op=ALU.mult)
                    I("pool", "tensor_scalar", [r_kkn], [r_nkk], out=nkk[:, 0:N], in0=kkn[:, 0:N], scalar1=-1.0,
                      scalar2=None, op0=ALU.mult)
                    for d in range(2):
                        I("dve", "tensor_scalar", [r_av, r_cst, r_lob], [r_tmpb], out=tmpb[:, 0:N], in0=av[:, d, 0:N],
                          scalar1=hp(h, 5), scalar2=omka[:, h:h + 1], op0=ALU.mult, op1=ALU.add)
                        I("dve", "tensor_tensor", [r_tmpb, r_kk], [r_kd], out=kd[:, d, 0:N], in0=tmpb[:, 0:N],
                          in1=kk_[:, 0:N], op=ALU.mult)
                        I("pool", "tensor_tensor", [r_kkn, r_av], [r_bv], out=bv[:, d, 0:N], in0=kkn[:, 0:N],
                          in1=av[:, d, 0:N], op=ALU.mult)
                        I("dve", "tensor_tensor_scan", [r_sg, r_cst], [r_Pc], out=Pc[:, d, 0:N],
                          data0=cst[0:64, C_RS:C_RS + N], data1=sg[:, d, 0:N], initial=0.0, op0=ALU.mult, op1=ALU.add)
                    P3 = lambda d: c3(Pc[:, d, 0:N], nc_)
                    tot = lambda d: P3(d)[:, :, 63:64]
                    totb = lambda d: tot(d).broadcast_to([64, nc_, 64])
                    v3 = lambda ap: c3(ap, nc_)
                    I("act", "activation", [r_Pc], [r_Ein], out=Ein[:, 0, 0:N], in_=Pc[:, 0, 0:N], func=ACTF.Exp, scale=CNEG)
                    I("act", "activation", [r_Pc], [r_Eng], out=Eng[:, 0, 0:N], in_=Pc[:, 0, 0:N], func=ACTF.Exp, scale=-CNEG)
                    I("dve", "tensor_tensor", [r_Pc, r_sg], [r_tmpa], out=tmpa[:, 0:N], in0=Pc[:, 0, 0:N], in1=sg[:, 0, 0:N],
                      op=ALU.subtract)
                    I("act", "activation", [r_tmpa], [r_Eex], out=Eex[:, 0, 0:N], in_=tmpa[:, 0:N], func=ACTF.Exp, scale=CNEG)
                    I("dve", "tensor_tensor", [r_Pc], [r_tmpb], out=v3(tmpb[:, 0:N]), in0=totb(0), in1=P3(0),
                      op=ALU.subtract)
                    I("act", "activation", [r_tmpb], [r_EC], out=EC[:, 0, 0:N], in_=tmpb[:, 0:N], func=ACTF.Exp, scale=CNEG)
                    I("dve", "tensor_tensor", [r_Pc], [r_tmpa], out=v3(tmpa[:, 0:N]), in0=totb(1), in1=P3(1),
                      op=ALU.subtract)
                    I("act", "activation", [r_tmpa], [r_Eex], out=Eex[:, 1, 0:N], in_=tmpa[:, 0:N], func=ACTF.Exp, scale=CNEG)
                    I("dve", "tensor_tensor", [r_tmpa, r_sg], [r_tmpa], out=tmpa[:, 0:N], in0=tmpa[:, 0:N], in1=sg[:, 1, 0:N],
                      op=ALU.add)
                    I("act", "activation", [r_tmpa], [r_Ein], out=Ein[:, 1, 0:N], in_=tmpa[:, 0:N], func=ACTF.Exp, scale=CNEG)
                    I("act", "activation", [r_tmpa], [r_Eng], out=Eng[:, 1, 0:N], in_=tmpa[:, 0:N], func=ACTF.Exp, scale=-CNEG)
                    I("dve", "tensor_tensor", [r_Pc, r_sg], [r_tmpb], out=tmpb[:, 0:N], in0=Pc[:, 1, 0:N], in1=sg[:, 1, 0:N],
                      op=ALU.subtract)
                    I("act", "activation", [r_tmpb], [r_EC], out=EC[:, 1, 0:N], in_=tmpb[:, 0:N], func=ACTF.Exp, scale=CNEG)
                    for d in range(2):
                        I("act", "activation", [r_Pc], [r_gam], out=gam[:, d, 0:nc_].unsqueeze(2), in_=tot(d), func=ACTF.Exp,
                          scale=CNEG)
                    for d in range(2):
                        e1, e2 = ("dve", "pool") if d == 0 else ("pool", "dve")
                        I(e1, "tensor_tensor", [r_nkk, r_Eex], [r_AR], out=AR[:, 0:nc_, 0, d, :], in0=v3(nkk[:, 0:N]),
                          in1=v3(Eex[:, d, 0:N]), op=ALU.mult)
                        I(e2, "tensor_tensor", [r_rr, r_Ein], [r_AR], out=AR[:, 0:nc_, 1, d, :], in0=v3(rr[:, 0:N]),
                          in1=v3(Ein[:, d, 0:N]), op=ALU.mult)
                        I(e1, "tensor_tensor", [r_bv, r_Eng], [r_Bt], out=Bt[:, 0:nc_, d, :], in0=v3(bv[:, d, 0:N]),
                          in1=v3(Eng[:, d, 0:N]), op=ALU.mult)
                        I(e2, "tensor_tensor", [r_kd, r_Eng], [r_Kt], out=Kt[:, 0:nc_, d, :], in0=v3(kd[:, d, 0:N]),
                          in1=v3(Eng[:, d, 0:N]), op=ALU.mult)
                        I(e1, "tensor_tensor", [r_bv, r_EC], [r_Bh], out=Bh[:, 0:nc_, d, :], in0=v3(bv[:, d, 0:N]),
                          in1=v3(EC[:, d, 0:N]), op=ALU.mult)
                        I(e2, "tensor_tensor", [r_kd, r_EC], [r_Kh], out=Kh[:, 0:nc_, d, :], in0=v3(kd[:, d, 0:N]),
                          in1=v3(EC[:, d, 0:N]), op=ALU.mult)
                        I("act", "activation", [r_vv], [r_Vd], out=Vd[:, 0:nc_, d, :], in_=v3(vv[:, 0:N]), func=ACTF.Copy)
                    I("dve", "tensor_tensor", [r_kd], [r_tmpa], out=tmpa[:, 0:N], in0=kd[:, 0, 0:N], in1=kd[:, 1, 0:N],
                      op=ALU.add)
                    I("dve", "scalar_tensor_tensor", [r_rr, r_cst, r_tmpa], [r_RK], out=RK[:, 0:nc_, :], in0=v3(rr[:, 0:N]),
                      scalar=hp(h, 6), in1=v3(tmpa[:, 0:N]), op0=ALU.mult, op1=ALU.mult)


                def unit_gen(c, par, h, hpar):
                    (AR, Bt, Kt, Bh, Kh, Vd, RK, gam, r_AR, r_Bt, r_Kt, r_Bh, r_Kh, r_Vd, r_RK, r_gam) = opsets[hpar]
                    bk_t, r_bk = BKbd[par]
                    vp_t, r_v = Vtm2[par]
                    v_t = vp_t[:, 64:128]
                    g_t, r_g = GBK[par]
                    py, r_py = psY[par]
                    w_t, r_wt = Wsb[par]
                    pa_t, r_pa, s_pa = oPA[par]
                    qb_t, r_qb, s_qb = oQB[par]
                    bg_t, r_bg, s_bg = oBG[par]
                    nlr, ycr = nlrots[par], ycrots[par]
                    fl = lambda ap: ap.rearrange("p d t -> p (d t)")
                    ARc = AR[:, c, :, :, :].rearrange("p a d t -> p (a d t)")
                    Atc = fl(AR[:, c, 0, :, :])
                    Rtc = fl(AR[:, c, 1, :, :])
                    Btc, Ktc = fl(Bt[:, c, :, :]), fl(Kt[:, c, :, :])
                    I("pe", "transpose", [r_Bh, r_ones], [r_psT], psT[:, 0, :], fl(Bh[:, c, :, :]), ident_bf[0:64, 0:64])
                    I("pe", "transpose", [r_Kh, r_ones], [r_psT], psT[:, 1, :], fl(Kh[:, c, :, :]), ident_bf[0:64, 0:64])
                    I("pe", "transpose", [r_Vd, r_ones], [r_psT], psT[:, 2, :], fl(Vd[:, c, :, :]), ident_bf[0:64, 0:64])
                    I("act", "activation", [r_psT], [r_bk], out=bk_t[0:64, :, 0:64], in_=psT[0:64, 0:2, :], func=ACTF.Copy)
                    I("act", "activation", [r_psT], [r_bk], out=bk_t[64:128, :, 64:128], in_=psT[64:128, 0:2, :],
                      func=ACTF.Copy)
                    I("act", "activation", [r_psT], [r_v], out=v_t, in_=psT[:, 2, :], func=ACTF.Copy)
                    yield
                    I("pe", "matmul", [r_Bt, r_AR], [r_psG], psG[:, 0:256], Btc, ARc, start=True, stop=True)
                    I("pe", "matmul", [r_Kt, r_AR], [r_psG], psG[:, 256:512], Ktc, ARc, start=True, stop=True)
                    I("pe", "matmul", [r_AR, r_Bt], [r_psL], psL[:, 1, :], Atc, Btc, start=True, stop=True)
                    I("dve", "tensor_tensor", [r_psG, r_cst], [r_g], out=g_t[:], in0=psG[:], in1=cst[:, C_MBK:C_MBK + 512],
                      op=ALU.mult)
                    nl, r_nl = nlr.next()
                    I("dve", "tensor_tensor", [r_psL, r_cst], [r_nl], out=nl[:, 1, :], in0=psL[:, 1, :],
                      in1=cst[:, C_ML:C_ML + 128], op=ALU.mult)
                    I("dve", "tensor_copy", [r_g], [r_nl], out=nl[:, 0, :], in_=g_t[:, 0:128])
                    yield
                    I("pe", "matmul", [r_AR, r_lob], [r_py], py[:], Atc, idpad[:], start=True, stop=True)
                    I("pe", "matmul", [r_g, r_v], [r_py], py[:], g_t[:, 256:384], vp_t[:], start=False, stop=True, skip_group_check=True)
                    for lvl in range(6):
                        yc, r_yc = ycr.next()
                        I("act", "activation", [r_py], [r_yc], out=yc[:], in_=py[:], func=ACTF.Copy)
                        I("pe", "matmul", [r_nl, r_yc], [r_py], py[:], nl[:, 0, :], yc[:], start=False, stop=True, skip_group_check=True)
                        if lvl < 5:
                            nl2, r_nl2 = nlr.next()
                            I("pe", "matmul", [r_nl], [r_psL], psL[:, 0, :], nl[:, 1, :], nl[:, 0, :], start=True, stop=True)
                            if lvl < 4:
                                I("pe", "matmul", [r_nl], [r_psL], psL[:, 1, :], nl[:, 0, :], nl[:, 1, :], start=True,
                                  stop=True)
                                I("dve", "tensor_copy", [r_psL], [r_nl2], out=nl2[:], in_=psL[:])
                            else:
                                I("dve", "tensor_copy", [r_psL], [r_nl2], out=nl2[:, 0, :], in_=psL[:, 0, :])
                            nl, r_nl = nl2, r_nl2
                        yield
                    I("act", "activation", [r_py], [r_wt], out=w_t[:], in_=py[:], func=ACTF.Copy)
                    I("pe", "matmul", [r_wt, r_g], [r_ps45], ps45[0:64, 0:128], w_t[:, 0:64], g_t[:, 128:256],
                      start=True, stop=True)
                    I("pe", "matmul", [r_wt, r_bk], [r_ps45], ps45[0:64, 128:256], w_t[:, 0:64], bk_t[:, 0, :],
                      start=True, stop=True)
                    I("pe", "matmul", [r_g, r_wt], [r_ps45], ps45[:, 256:320], g_t[:, 128:256], w_t[:, 64:128],
                      start=True, stop=False)
                    I("pe", "matmul", [r_g, r_v], [r_ps45], ps45[:, 256:320], g_t[:, 384:512], v_t,
                      start=False, stop=True)
                    I("pe", "matmul", [r_bk, r_wt], [r_ps45], ps45[:, 320:384], bk_t[:, 0, :], w_t[:, 64:128],
                      start=True, stop=False)
                    I("pe", "matmul", [r_bk, r_v], [r_ps45], ps45[:, 320:384], bk_t[:, 1, :], v_t,
                      start=False, stop=True)
                    I("pe", "matmul", [r_RK, r_ones], [r_ps45], ps45[0:64, 384:448], RK[:, c, :], ones64,
                      start=True, stop=True)
                    I("pe", "matmul", [r_sgb, r_lob], [r_ps45], ps45[0:64, 448:512], sgb[:, 0, c * 64:(c + 1) * 64],
                      gub[:, 0, h * 64:(h + 1) * 64], start=True, stop=False)
                    I("pe", "matmul", [r_sgb, r_lob], [r_ps45], ps45[0:64, 448:512], sgb[:, 1, c * 64:(c + 1) * 64],
                      gub[:, 1, h * 64:(h + 1) * 64], start=False, stop=True)
                    I("dve", "tensor_tensor", [r_ps45, r_AR], [r_pa], out=pa_t[:, 0:128], in0=ps45[0:64, 0:128],
                      in1=Rtc, op=ALU.add)
                    for d in range(2):
                        I("dve", "scalar_tensor_tensor", [r_cst, r_gam, r_ps45], [r_pa],
                          out=pa_t[:, 128 + d * 64:192 + d * 64], in0=cst[0:64, C_ID:C_ID + 64],
                          scalar=gam[:, d, c:c + 1], in1=ps45[0:64, 128 + d * 64:192 + d * 64], op0=ALU.mult, op1=ALU.add)
                    I("dve", "tensor_copy", [r_ps45], [r_qb], out=qb_t[:], in_=ps45[:, 256:384])
                    I("dve", "tensor_tensor", [r_ps45, r_v], [r_bg], out=bg_t[:, 0:64], in0=ps45[0:64, 384:448],
                      in1=vp_t[0:64, 64:128], op=ALU.mult)
                    I("dve", "tensor_copy", [r_ps45], [r_bg], out=bg_t[:, 64:128], in_=ps45[0:64, 448:512])
                    cg = ch0 + c
                    em.dma("sp", s_pa, PA[h][cg, :, :], pa_t[:], reads=[r_pa], writes=[r_PA[h]])
                    em.dma("sp", s_qb, QB[h][cg, :, :], qb_t[:], reads=[r_qb], writes=[r_QB[h]])
                    em.dma("sp", s_bg, BG[h][cg, :, :], bg_t[:], reads=[r_bg], writes=[r_BG[h]])
                    yield

                nh = DBG.get('p3h', 4)
                nu = min(nc_, DBG.get("units", 99))
                prep(0, 0)
                for h in range(nh):
                    rec = []
                    if h + 1 < nh:
                        REC[0] = rec
                        prep(h + 1, (h + 1) % 2)
                        REC[0] = None
                    npairs = (nu + 1) // 2
                    per = max(1, -(-len(rec) // max(1, (npairs * 9) // 2)))
                    pos = 0
                    for c0 in range(0, nu, 2):
                        live = [unit_gen(c0 + j, j, h, h % 2) for j in range(min(2, nu - c0))]
                        while live:
                            for g_ in list(live):
                                try:
                                    next(g_)
                                except StopIteration:
                                    live.remove(g_)
                            replay(rec[pos:pos + per])
                            pos += per
                    replay(rec[pos:])
            allr = [r_par, r_lob, r_twd, r_adz, r_sgb, r_zsh, r_rr, r_kk, r_vv, r_sg, r_av, r_kd, r_bv, r_Pc, r_kkn, r_nkk,
                    r_tmpa, r_tmpb, r_kq, r_Ein, r_Eng, r_Eex, r_EC,
                    r_psT, r_psG, r_psL, r_ps45] + r_PA + r_QB + r_BG + [x_ for o_ in opsets for x_ in o_[8:]] + \
                   [x[1] for x in BKbd + zl + psX + psY + Vtm2 + GBK + NL + Yc + Wsb + oPA + oQB + oBG]
            barrier(allr)

        if stop_after >= 4:
          with ExitStack() as ph:
            psb = lambda name, shape, dt=F32: ph.enter_context(nc.sbuf_tensor(name, list(shape), dt))
            pst = lambda name, shape, dt=F32: ph.enter_context(nc.psum_tensor(name, list(shape), dt))
            rbt = psb("rbt", [64, 4, 2, 64])
            r_rbt = R()
            s_rb = em.dsem("rb")
            em.dma("sp", s_rb, rbt[:], rowb[:, :, :, :], writes=[r_rbt])
            ysum = psb("ysum", [64, 2, 128, 64])
            r_y = [R(), R()]
            Hs = [[(psb("H%d_%d" % (d, i), [64, 64]), R()) for i in range(2)] for d in range(2)]
            pab = [[(psb("pab%d_%d" % (d, i), [64, 8, 2, 64]), R(), em.dsem("pab%d_%d" % (d, i))) for i in range(2)]
                   for d in range(2)]
            qbb = [[(psb("qbb%d_%d" % (d, i), [64, 8, 2, 64]), R(), em.dsem("qbb%d_%d" % (d, i))) for i in range(2)]
                   for d in range(2)]
            psH = [(pst("psH%d" % d, [64, 64]), R()) for d in range(2)]
            psYY = [(pst("psYY%d" % d, [64, 64]), R()) for d in range(2)]
            psTr = pst("psTr", [64, 512]); r_psTr = R()
            bgb = psb("bgb", [64, 32, 2, 64]); r_bgb = R(); s_bgb = em.dsem("bgb")
            yt = psb("yt", [64, 32, 64]); r_yt = R()
            yt2 = psb("yt2", [64, 32, 64]); r_yt2 = R()
            st_ = psb("st_", [64, 4, 32]); r_st = R()
            rwst = [(psb("rwst%d" % i, [64, 512], BF16), R(), em.dsem("rwst%d" % i)) for i in range(2)]
            rwi = 0
            for h in range(DBG.get('p3h', 4)):
                batches = {0: [list(range(0, 4))] + [list(range(4 + 8 * k, 12 + 8 * k)) for k in range(16)],
                           1: [[3, 2, 1, 0]] + [list(range(4 + 8 * k + 7, 4 + 8 * k - 1, -1)) for k in range(15, -1, -1)]}
                hcur = [0, 0]
                for d in range(2):
                    I("dve", "memset", [], [Hs[d][0][1]], Hs[d][0][0][:], 0.0)
                for bi in range(17):
                    loaded = {}
                    for d in range(2):
                        ids = batches[d][bi]
                        lo, n = min(ids), len(ids)
                        pa_t, r_pa, s_pa = pab[d][bi % 2]
                        qb_t, r_qb, s_qb = qbb[d][bi % 2]
                        for x_ in range(2):
                            em.dma("sp", s_pa, pa_t[:, 0:n, x_, :],
                                   PA[h][lo:lo + n, :, x_ * 128 + d * 64:x_ * 128 + d * 64 + 64].rearrange("c k j -> k c j"),
                                   reads=[r_PA[h]], writes=[r_pa])
                        em.dma("sp", s_qb, qb_t[:, 0:n, :, :],
                               QB[h][lo:lo + n, d * 64:(d + 1) * 64, :].rearrange("c p (x j) -> p c x j", x=2),
                               reads=[r_QB[h]], writes=[r_qb])
                        loaded[d] = (ids, lo, pa_t, r_pa, qb_t, r_qb)
                    n = len(batches[0][bi])
                    for step in range(n):
                        for d in range(2):
                            ids, lo, pa_t, r_pa, qb_t, r_qb = loaded[d]
                            cid = ids[step]
                            j = cid - lo
                            Hc, r_Hc = Hs[d][hcur[d] % 2]
                            Hn, r_Hn = Hs[d][(hcur[d] + 1) % 2]
                            hcur[d] += 1
                            pH, r_pH = psH[d]
                            if cid >= 4:
                                pY, r_pY = psYY[d]
                                I("pe", "matmul", [r_pa, r_Hc], [r_pY], pY[:], pa_t[:, j, 0, :], Hc[:], start=True, stop=True)
                                I("dve", "tensor_tensor", [r_pY, r_qb], [r_y[d]], out=ysum[:, d, cid - 4, :], in0=pY[:],
                                  in1=qb_t[:, j, 0, :], op=ALU.add)
                            I("pe", "matmul", [r_pa, r_Hc], [r_pH], pH[:], pa_t[:, j, 1, :], Hc[:], start=True, stop=True)
                            I("dve", "tensor_tensor", [r_pH, r_qb], [r_Hn], out=Hn[:], in0=pH[:], in1=qb_t[:, j, 1, :],
                              op=ALU.add)
                for qtr in range(4):
                    c0 = qtr * 32
                    em.dma("sp", s_bgb, bgb[:], BG[h][4 + c0:4 + c0 + 32, :, :].rearrange("c p (x j) -> p c x j", x=2),
                           reads=[r_BG[h]], writes=[r_bgb])
                    I("dve", "tensor_tensor", r_y, [r_yt], out=yt[:], in0=ysum[:, 0, c0:c0 + 32, :],
                      in1=ysum[:, 1, c0:c0 + 32, :], op=ALU.add)
                    I("dve", "reduce_sum", [r_yt], [r_st], out=st_[:, 0, :], in_=yt[:], axis=AX.X)
                    I("dve", "tensor_scalar", [r_st], [r_st], out=st_[:, 0, :], in0=st_[:, 0, :], scalar1=1.0 / 64,
                      scalar2=None, op0=ALU.mult)
                    I("dve", "tensor_tensor", [r_yt, r_st], [r_yt], out=yt[:], in0=yt[:],
                      in1=st_[:, 0, :].unsqueeze(2).broadcast_to([64, 32, 64]), op=ALU.subtract)
                    I("pool", "tensor_tensor", [r_yt], [r_yt2], out=yt2[:], in0=yt[:], in1=yt[:], op=ALU.mult)
                    I("dve", "reduce_sum", [r_yt2], [r_st], out=st_[:, 1, :], in_=yt2[:], axis=AX.X)
                    I("dve", "tensor_scalar", [r_st], [r_st], out=st_[:, 1, :], in0=st_[:, 1, :], scalar1=1.0 / 64,
                      scalar2=64e-5, op0=ALU.mult, op1=ALU.add)
                    I("act", "activation", [r_st], [r_st], out=st_[:, 1, :], in_=st_[:, 1, :], func=ACTF.Sqrt)
                    I("dve", "reciprocal", [r_st], [r_st], out=st_[:, 1, :], in_=st_[:, 1, :])
                    I("dve", "tensor_tensor", [r_yt, r_st], [r_yt], out=yt[:], in0=yt[:],
                      in1=st_[:, 1, :].unsqueeze(2).broadcast_to([64, 32, 64]), op=ALU.mult)
                    I("dve", "tensor_tensor", [r_yt, r_rbt], [r_yt], out=yt[:], in0=yt[:],
                      in1=rbt[:, h, 0, :].unsqueeze(1).broadcast_to([64, 32, 64]), op=ALU.mult)
                    I("dve", "tensor_tensor", [r_yt, r_rbt], [r_yt], out=yt[:], in0=yt[:],
                      in1=rbt[:, h, 1, :].unsqueeze(1).broadcast_to([64, 32, 64]), op=ALU.add)
                    I("dve", "tensor_tensor", [r_yt, r_bgb], [r_yt], out=yt[:], in0=yt[:], in1=bgb[:, :, 0, :], op=ALU.add)
                    I("dve", "tensor_tensor", [r_yt, r_bgb], [r_yt], out=yt[:], in0=yt[:], in1=bgb[:, :, 1, :], op=ALU.mult)
                    for g8 in range(4):
                        rw_t, r_rw, s_rw = rwst[rwi % 2]
                        rwi += 1
                        for j in range(8):
                            I("pe", "transpose", [r_yt, r_cst], [r_psTr], psTr[:, j * 64:(j + 1) * 64], yt[:, g8 * 8 + j, :],
                              ident_f[0:64, 0:64])
                        I("act", "activation", [r_psTr], [r_rw], out=rw_t[:], in_=psTr[:], func=ACTF.Copy)
                        tok0 = (c0 + g8 * 8) * 64
                        qd, col = tok0 // TOK, tok0 % TOK
                        em.dma("sp", s_rw, mixL[qd * 512 + 256 + h * 64:qd * 512 + 256 + h * 64 + 64, col:col + 512], rw_t[:],
                               reads=[r_rw], writes=[r_mixL])
            barrier([r_mixL, r_rbt, r_bgb, r_yt, r_yt2, r_st, r_psTr] + r_y +
                    [x[1] for x in Hs[0] + Hs[1] + pab[0] + pab[1] + qbb[0] + qbb[1] + psH + psYY + rwst])

        if debug:
            mixD = nc.dram_tensor("dbg_mixL", [4 * 512, TOK], BF16, kind="ExternalOutput")
            em.dma("sp", s_dbg, mixD.ap(), mixL.ap(), reads=[r_mixL], writes=[r_dbg])
        if stop_after >= 5:
            s_cc = em.sems["cc"] = nc.alloc_semaphore("cc")
            em.wait_all("pool", [r_mixL])
            for k in range(8):
                em.raw("pool", lambda e, k=k: e.collective_compute(
                    "AllGather", ALU.bypass, replica_groups=[[0, 1, 2, 3], [4, 5, 6, 7]],
                    ins=[mixL[k * 256:(k + 1) * 256, :]], outs=[mixG[k * 1024:(k + 1) * 1024, :]]).then_inc(s_cc))
            em.raw("pool", lambda e: e.wait_ge(s_cc, 8))
            I("pool", "memset", [], [r_mixG], ones_bf[0:1, 0:1], 1.0)

        if stop_after >= 6:
          with ExitStack() as ph:
            psb = lambda name, shape, dt=F32: ph.enter_context(nc.sbuf_tensor(name, list(shape), dt))
            pst = lambda name, shape, dt=F32: ph.enter_context(nc.psum_tensor(name, list(shape), dt))
            mixT = psb("mixT", [128, KC, 512], BF16); r_mixT = R(); s_mix = em.dsem("mix")
            x1 = psb("x1", [128, KC, 512]); r_x1 = R(); s_x1 = em.dsem("x1")
            sq = psb("sq2", [128, KC, 512], BF16); r_sq = R()
            h2, r_h2 = mixT, r_mixT
            aT = psb("aT", [128, 44, 512], BF16); r_aT = R()
            rstd = psb("rstd2", [128, 512]); r_rstd = R()
            su = psb("su", [128, 512]); r_su = R()
            wf = [(psb("wf%d" % i, [128, 16, 128]), R(), em.dsem("wf%d" % i)) for i in range(3)]
            wfrot = Rot(wf)
            wbf = [(psb("wbf%d" % i, [128, 16, 128], BF16), R()) for i in range(4)]
            wbrot = Rot(wbf)
            ost = [(psb("ost6_%d" % i, [128, 512]), R(), em.dsem("ost6_%d" % i)) for i in range(2)]
            orot = Rot(ost)
            pss = [(pst("ps6_%d" % i, [128, 512]), R()) for i in range(4)]
            psrot = Rot(pss)
            ps_ss = pst("ps_ss6", [128, 512]); r_psss = R()
            gcache = {}

            def gi():
                if "v" not in gcache:
                    gcache["v"] = nc.gpsimd.partition_id() % 4
                return gcache["v"]
            wo_v = w_out.ap().rearrange("(kc p) n -> p kc n", p=128)
            w1_v = w1.ap().rearrange("(kc p) n -> p kc n", p=128)
            w3_v = w3.ap().rearrange("(kc p) n -> p kc n", p=128)
            w2_v = w2.ap().rearrange("(kc p) n -> p kc n", p=128)
            xk_v = xtok.ap().rearrange("(kc p) t -> p kc t", p=128)
            mg_v = mixG.ap()
            o_v = outT.ap().rearrange("(kc p) t -> p kc t", p=128)
            cast_i = [0]

            def load_w(view, nk, c0, k0=0):
                w_t, r_w, s_w = wfrot.next()
                wb_t, r_wb = wbrot.next()
                em.dma("sp", s_w, w_t[:, 0:nk, :], view[:, k0:k0 + nk, c0:c0 + 128], writes=[r_w])
                eng = ("pool", "pool", "act")[cast_i[0] % 3]
                cast_i[0] += 1
                if eng == "act":
                    I("act", "activation", [r_w], [r_wb], out=wb_t[:, 0:nk, :], in_=w_t[:, 0:nk, :], func=ACTF.Copy)
                else:
                    I(eng, "tensor_copy", [r_w], [r_wb], out=wb_t[:, 0:nk, :], in_=w_t[:, 0:nk, :])
                return wb_t, r_wb

            def norm_stats():
                for kc in range(KC):
                    I("pe", "matmul", [r_sq, r_ones], [r_psss], ps_ss[:], ones_bf[:], sq[:, kc, :], start=(kc == 0),
                      stop=(kc == KC - 1))
                I("dve", "tensor_scalar", [r_psss], [r_rstd], out=rstd[:], in0=ps_ss[:], scalar1=1.0 / D, scalar2=1e-6,
                  op0=ALU.mult, op1=ALU.add)
                I("act", "activation", [r_rstd], [r_rstd], out=rstd[:], in_=rstd[:], func=ACTF.Sqrt)
                I("dve", "reciprocal", [r_rstd], [r_rstd], out=rstd[:], in_=rstd[:])

            for tt in range(4):
                for hf in range(2):
                    em.dma("pool", s_mix, mixT[:, hf * 8:(hf + 1) * 8, :],
                           (lambda tt=tt, hf=hf: mixG[bass.ds(gi() * 2048 + hf * 1024, 1024), tt * 512:(tt + 1) * 512]
                            .rearrange("(k p) t -> p k t", p=128)),
                           reads=[r_mixG], writes=[r_mixT])
                for half in range(2):
                    em.dma("sp", s_x1, x1[:, half * 8:(half + 1) * 8, :],
                           xk_v[:, half * 8:(half + 1) * 8, tt * 512:(tt + 1) * 512], writes=[r_x1])
                for n in range(KC):
                    wb_t, r_wb = load_w(wo_v, KC, n * 128)
                    ps, r_ps = psrot.next()
                    for kc in range(KC):
                        I("pe", "matmul", [r_wb, r_mixT], [r_ps], ps[:], wb_t[:, kc, :], mixT[:, kc, :], start=(kc == 0),
                          stop=(kc == KC - 1))
                    I("dve", "scalar_tensor_tensor", [r_ps, r_modT, r_x1], [r_x1], out=x1[:, n, :], in0=ps[:],
                      scalar=modT[:, 32 + n, 0:1], in1=x1[:, n, :], op0=ALU.mult, op1=ALU.add)
                    I("act", "activation", [r_x1], [r_sq], out=sq[:, n, :], in_=x1[:, n, :], func=ACTF.Square)
                norm_stats()
                for kc in range(KC):
                    I("dve", "tensor_tensor", [r_x1, r_rstd], [r_su], out=su[:], in0=x1[:, kc, :], in1=rstd[:], op=ALU.mult)
                    I("act", "activation", [r_su, r_A2, r_modT], [r_h2], out=h2[:, kc, :], in_=su[:], func=ACTF.Identity,
                      scale=A2[:, kc:kc + 1], bias=modT[:, 48 + kc, 0:1])
                for f in range(44):
                    w1b, r_w1b = load_w(w1_v, KC, f * 128)
                    w3b, r_w3b = load_w(w3_v, KC, f * 128)
                    p1, r_p1 = psrot.next()
                    p3, r_p3 = psrot.next()
                    for kc in range(KC):
                        I("pe", "matmul", [r_w1b, r_h2], [r_p1], p1[:], w1b[:, kc, :], h2[:, kc, :], start=(kc == 0),
                          stop=(kc == KC - 1))
                    for kc in range(KC):
                        I("pe", "matmul", [r_w3b, r_h2], [r_p3], p3[:], w3b[:, kc, :], h2[:, kc, :], start=(kc == 0),
                          stop=(kc == KC - 1))
                    I("act", "activation", [r_p1], [r_su], out=su[:], in_=p1[:], func=ACTF.Silu)
                    I("dve", "tensor_tensor", [r_su, r_p3], [r_aT], out=aT[:, f, :], in0=su[:], in1=p3[:], op=ALU.mult)
                for n in range(KC):
                    ps, r_ps = psrot.next()
                    for (k0, nk) in ((0, 16), (16, 16), (32, 12)):
                        wb_t, r_wb = load_w(w2_v, nk, n * 128, k0)
                        for f in range(nk):
                            I("pe", "matmul", [r_wb, r_aT], [r_ps], ps[:], wb_t[:, f, :], aT[:, k0 + f, :],
                              start=(k0 + f == 0), stop=(k0 + f == 43))
                    I("dve", "scalar_tensor_tensor", [r_ps, r_modT, r_x1], [r_x1], out=x1[:, n, :], in0=ps[:],
                      scalar=modT[:, 80 + n, 0:1], in1=x1[:, n, :], op0=ALU.mult, op1=ALU.add)
                    I("act", "activation", [r_x1], [r_sq], out=sq[:, n, :], in_=x1[:, n, :], func=ACTF.Square)
                norm_stats()
                for kc in range(KC):
                    o_t, r_o, s_o = orot.next()
                    I("dve", "scalar_tensor_tensor", [r_x1, r_cst, r_rstd], [r_o], out=o_t[:], in0=x1[:, kc, :],
                      scalar=ppt[:, P_FG + kc:P_FG + kc + 1], in1=rstd[:], op0=ALU.mult, op1=ALU.mult)
                    em.dma("sp", s_o, o_v[:, kc, tt * 512:(tt + 1) * 512], o_t[:], reads=[r_o], writes=[r_dbg])
            barrier([r_dbg, r_x1, r_mixT, r_sq, r_h2, r_aT, r_rstd, r_su, r_psss] +
                    [x[1] for x in wf + wbf + ost + pss])

        if DBG.get('dummy'):
            dmy = top.enter_context(nc.sbuf_tensor("dmy", [128, 8], F32))
            r_dmy = R()
            for i in range(DBG['dummy']):
                em.ops["pool"].append(lambda e: e.memset(dmy[:], 1.0))
        barrier([r_dbg])
        with nc.Block() as block:
            em.run(block)
    return nc


def _consts():
    c = np.zeros((128, NCONST), np.float32)
    c[:, C_ID:C_ID + 128] = np.eye(128, dtype=np.float32)
    q = np.arange(128)[:, None]
    k = np.arange(128)[None, :]
    c[:, C_MP:C_MP + 128] = np.where(k >= q, 0.0, NEG)
    c[:, C_MN:C_MN + 128] = np.where(k <= q, 0.0, NEG)
    c[:, C_MF:C_MF + 128] = NEG
    s = np.arange(64)[:, None]
    t = np.arange(64)[None, :]
    strictN = np.zeros((128, 128), np.float32)
    inclN = np.zeros((128, 128), np.float32)
    strictN[0:64, 0:64] = (s < t)
    strictN[64:128, 64:128] = (s > t)
    inclN[0:64, 0:64] = (s <= t)
    inclN[64:128, 64:128] = (s >= t)
    c[:, C_MBK:C_MBK + 512] = np.concatenate([strictN, inclN, strictN, inclN], axis=1)
    c[:, C_ML:C_ML + 128] = strictN.T
    rs = np.ones((128, 512), np.float32)
    rs[:, ::64] = 0.0
    c[:, C_RS:C_RS + 512] = rs
    return c


def _rope_tables():
    nf = 16
    freqs = (1.0 / (10000.0 ** (np.arange(nf, dtype=np.float32) / nf))).astype(np.float32)
    t = np.arange(SEQ)
    row = (t // 64).astype(np.float32)
    col = (t % 64).astype(np.float32)
    tab = np.zeros((64, 2, SEQ), np.float32)
    for j in range(64):
        pos = row if j < 32 else col
        ang = pos * freqs[j % 16]
        sign = -1.0 if (j % 32) < 16 else 1.0
        tab[j, 0] = np.cos(ang)
        tab[j, 1] = sign * np.sin(ang)
    return tab


def _prep_inputs(inp):
    f = lambda a: np.ascontiguousarray(a, dtype=np.float32)
    x, c, ctx, c_ctx = inp["x"], inp["c"], inp["ctx"], inp["c_ctx"]
    w_in = inp["w_in"][0]
    perm64 = np.concatenate([np.arange(16, 32), np.arange(0, 16), np.arange(48, 64), np.arange(32, 48)])
    consts = _consts()
    rope = _rope_tables()
    wo = inp["w_out"][0]
    rows = np.concatenate([np.concatenate([np.arange(r * 256, (r + 1) * 256), 1024 + np.arange(r * 256, (r + 1) * 256)])
                           for r in range(4)])
    wo_p = f(wo)
    w1, w3, w2 = f(inp["ffn_w1"][0]), f(inp["ffn_w3"][0]), f(inp["ffn_w2"][0])
    ada_w = f(inp["ada_w"][0])
    ada_bT = f(inp["ada_b"][0].reshape(96, 128).T)
    n1g = f(inp["norm1_g"][0].reshape(KC, 128).T)
    xT_b = [f(np.concatenate([ctx[b], x[b]], axis=0).T) for b in range(2)]
    tsp, tsn = inp["ts_prev"][0], inp["ts_next"][0]
    maps = []
    for core in range(8):
        b, g = core // 4, core % 4
        cols = []
        qc = np.arange(g * 256, (g + 1) * 256)
        cols.append(qc)
        cols.append((qc.reshape(4, 64)[:, perm64]).reshape(-1))
        kc_ = 1024 + np.arange(g * 64, (g + 1) * 64)
        cols.append(kc_)
        cols.append(kc_[perm64])
        cols.append(1280 + np.arange(g * 64, (g + 1) * 64))
        base = 1536
        for i in range(3):
            cols.append(base + i * 1024 + np.arange(g * 256, (g + 1) * 256))
        cols.append(base + 3072 + np.arange(0, 416))
        cols = np.concatenate(cols)
        pp = np.zeros((128, NPP), np.float32)
        pp[:, P_SINK:P_SINK + 4] = inp["attn_sink"][0][g * 4:(g + 1) * 4][None, :]
        grp_idx = []
        for i in range(3):
            for h in range(4):
                grp_idx.append(i * 1024 + g * 256 + h * 64 + np.arange(64))
        for i in range(4):
            grp_idx.append(3072 + i * 64 + np.arange(64))
        grp_idx.append(3328 + np.arange(128))
        grp_idx.append(3328 + 128 + np.arange(32))
        for gi_, idx in enumerate(grp_idx):
            pp[0:len(idx), P_TS + gi_] = tsp[idx]
            pp[0:len(idx), P_TS + 18 + gi_] = tsn[idx]
        hpar = [inp["w0"][0][0], inp["w0"][0][1], inp["a0"][0][0], inp["a0"][0][1], inp["k_k"][0], inp["k_a"][0],
                inp["r_k"][0]]
        for i, arr in enumerate(hpar):
            for h in range(4):
                pp[0:64, P_HP + i * 4 + h] = arr[g * 256 + h * 64:g * 256 + (h + 1) * 64]
        pp[:, P_N2:P_N2 + 16] = inp["norm2_g"][0].reshape(KC, 128).T
        pp[:, P_FG:P_FG + 16] = inp["final_norm_g"].reshape(KC, 128).T
        rowb = np.zeros((64, 4, 2, 64), np.float32)
        for h in range(4):
            rowb[:, h, 0, :] = inp["lnx_g"][0][g * 256 + h * 64:g * 256 + (h + 1) * 64][None, :]
            rowb[:, h, 1, :] = inp["lnx_b"][0][g * 256 + h * 64:g * 256 + (h + 1) * 64][None, :]
        lora = np.zeros((64, 2, 2, 4, 64), np.float32)
        for d in range(2):
            lora[:, 0, d] = inp["w_up"][0][d][:, g * 256:(g + 1) * 256].reshape(64, 4, 64)
            lora[:, 1, d] = inp["a_up"][0][d][:, g * 256:(g + 1) * 256].reshape(64, 4, 64)
        m = {
            "xT": xT_b[b],
            "xtok": f(xT_b[b][:, CTX + g * TOK:CTX + (g + 1) * TOK]),
            "c_col": f(np.stack([c[b], c_ctx], axis=1).reshape(KC, 128, 2).transpose(1, 0, 2)),
            "ada_w": ada_w, "ada_bT": ada_bT, "n1g": n1g,
            "Wc": f(w_in[:, cols]),
            "consts": consts, "pp": pp, "rope": rope, "rowb": rowb, "lora": lora,
            "gup": f(inp["g_up"][0][:, g * 256:(g + 1) * 256]),
            "w_out": wo_p, "w1": w1, "w3": w3, "w2": w2,
        }
        maps.append(m)
    return maps


_NC_CACHE = {}


def kernel(**inputs):
    inputs = {k: np.asarray(v) for k, v in inputs.items()}
    maps = _prep_inputs(inputs)
    if "nc" not in _NC_CACHE:
        _NC_CACHE["nc"] = build_program()
    nc = _NC_CACHE["nc"]
    res = run_bass_kernel_spmd(nc, maps, core_ids=list(range(8)))
    out = np.zeros((2, SEQ, D), np.float32)
    for core in range(8):
        b, g = core // 4, core % 4
        out[b, g * TOK:(g + 1) * TOK, :] = np.asarray(res.results[core]["outT"]).T
    return out
```

```python
import numpy as np
import concourse.bass as bass
import concourse.mybir as mybir
from concourse.bass_utils import run_bass_kernel_spmd

F32 = mybir.dt.float32
BF16 = mybir.dt.bfloat16
ALU = mybir.AluOpType
ACTF = mybir.ActivationFunctionType
AX = mybir.AxisListType

D = 2048
SEQ = 8192
CTX = 256
TT = SEQ + CTX
KC = D // 128
NA = 704
NR = 1184
NCOL = NA + NR
DFF = 5632
CH = 64
NCH = TT // CH
TOK = 2048


class Region:
    __slots__ = ("name", "w", "r")

    def __init__(self, name):
        self.name = name
        self.w = {}
        self.r = {}


class Emitter:
    def __init__(self, nc):
        self.nc = nc
        self.eng = {}
        self.ops = {}
        self.sems = {}
        self.semtot = {}
        for name, obj in (("pe", nc.tensor), ("dve", nc.vector), ("act", nc.scalar),
                          ("pool", nc.gpsimd), ("sp", nc.sync)):
            self.eng[name] = obj
            self.ops[name] = []
            self.sems[name] = nc.alloc_semaphore("e_" + name)
            self.semtot[name] = 0
        self.seen = {n: {} for n in self.eng}
        self.nreg = 0

    def region(self, name=None):
        self.nreg += 1
        return Region(name or "r%d" % self.nreg)

    def dsem(self, name):
        key = "d_" + name
        self.sems[key] = self.nc.alloc_semaphore(key)
        self.semtot[key] = 0
        return key

    def _deps(self, reads, writes):
        deps = {}
        for r in reads:
            for k, v in r.w.items():
                deps[k] = max(deps.get(k, 0), v)
        for w in writes:
            for k, v in w.w.items():
                deps[k] = max(deps.get(k, 0), v)
            for k, v in w.r.items():
                deps[k] = max(deps.get(k, 0), v)
        return deps

    def _emit_waits(self, e, deps):
        seen = self.seen[e]
        for k, v in deps.items():
            if k == e and e == "pe":
                continue
            if k.startswith("d_"):
                v = self.semtot[k]
            if seen.get(k, 0) >= v:
                continue
            seen[k] = v
            sem = self.sems[k]
            self.ops[e].append(lambda eng, sem=sem, v=v: eng.wait_ge(sem, v))

    def _record(self, reads, writes, key, val):
        for r in reads:
            r.r[key] = max(r.r.get(key, 0), val)
        for w in writes:
            w.w = {key: val}
            w.r = {}

    def op(self, e, fn, reads=(), writes=()):
        deps = self._deps(reads, writes)
        self._emit_waits(e, deps)
        self.semtot[e] += 1
        val = self.semtot[e]
        sem = self.sems[e]
        self.ops[e].append(lambda eng, fn=fn, sem=sem: fn(eng).then_inc(sem, 1))
        self._record(reads, writes, e, val)

    def dma(self, q, semkey, out, in_, reads=(), writes=(), **kw):
        deps = self._deps(reads, writes)
        self._emit_waits(q, deps)
        self.semtot[semkey] += 16
        val = self.semtot[semkey]
        sem = self.sems[semkey]
        self.ops[q].append(lambda eng, sem=sem, out=out, in_=in_, kw=kw:
                           eng.dma_start(out=out, in_=(in_() if callable(in_) else in_), **kw).then_inc(sem, 16))
        self._record(reads, writes, semkey, val)

    def raw(self, e, fn):
        self.ops[e].append(fn)

    def wait_all(self, e, regions):
        deps = {}
        for r in regions:
            for k, v in list(r.w.items()) + list(r.r.items()):
                deps[k] = max(deps.get(k, 0), v)
        self._emit_waits(e, deps)

    def run(self, block):
        ops = self.ops

        @block.tensor
        def _(eng):
            for f in ops["pe"]:
                f(eng)

        @block.vector
        def _(eng):
            for f in ops["dve"]:
                f(eng)

        @block.scalar
        def _(eng):
            for f in ops["act"]:
                f(eng)

        @block.gpsimd
        def _(eng):
            for f in ops["pool"]:
                f(eng)

        @block.sync
        def _(eng):
            for f in ops["sp"]:
                f(eng)


class Rot:
    def __init__(self, items):
        self.items = items
        self.i = 0

    def next(self):
        it = self.items[self.i % len(self.items)]
        self.i += 1
        return it


DBG = {}
CNEG = -0.6065306597126334
NEG = -1.0e30
C_ID = 0
C_MP = 128
C_MN = 256
C_MF = 384
C_MBK = 512
C_ML = 1024
C_RS = 1152
NCONST = 1664
P_SINK = 0
P_TS = 4
P_HP = 40
P_N2 = 68
P_FG = 84
NPP = 100


def build_program(stop_after=99, debug=False):
    from contextlib import ExitStack
    nc = bass.Bass("TRN2", target_bir_lowering=False)
    em = Emitter(nc)
    R = em.region

    def din(name, shape, dt=F32):
        return nc.dram_tensor(name, list(shape), dt, kind="ExternalInput")

    def dscr(name, shape, dt=F32):
        if debug:
            return nc.dram_tensor(name, list(shape), dt, kind="ExternalOutput")
        return nc.dram_tensor(name, list(shape), dt)

    def I(eng, method, r, w, *args, **kw):
        em.op(eng, lambda e: getattr(e, method)(*args, **kw), r, w)

    xT = din("xT", [D, TT])
    xtok = din("xtok", [D, TOK])
    c_col = din("c_col", [128, KC, 2])
    ada_w = din("ada_w", [D, 6 * D])
    ada_bT = din("ada_bT", [128, 96])
    n1g = din("n1g", [128, KC])
    Wc = din("Wc", [D, NCOL])
    consts = din("consts", [128, NCONST])
    pp = din("pp", [128, NPP])
    rope = din("rope", [64, 2, SEQ])
    rowb = din("rowb", [64, 4, 2, 64])
    lora = din("lora", [64, 2, 2, 4, 64])
    gup = din("gup", [160, 256])
    w_out = din("w_out", [D, D])
    w1 = din("w1", [D, DFF])
    w3 = din("w3", [D, DFF])
    w2 = din("w2", [DFF, D])
    outT = nc.dram_tensor("outT", [D, TOK], F32, kind="ExternalOutput")
    zA = dscr("zA", [NA, TT], BF16)
    zR = dscr("zR", [NR, TT], F32)
    PA = [nc.dram_tensor("PA%d" % h, [NCH, 64, 256], F32) for h in range(4)]
    QB = [nc.dram_tensor("QB%d" % h, [NCH, 128, 128], F32) for h in range(4)]
    BG = [nc.dram_tensor("BG%d" % h, [NCH, 64, 128], F32) for h in range(4)]
    mixL = nc.dram_tensor("mixL", [4 * 512, TOK], BF16)
    mixG = nc.dram_tensor("mixG", [4 * 4 * 512, TOK], BF16)
    r_zA, r_zR, r_mixL, r_mixG = R("zA"), R("zR"), R("mixL"), R("mixG")
    r_PA = [R() for _ in range(4)]
    r_QB = [R() for _ in range(4)]
    r_BG = [R() for _ in range(4)]

    s_dbg = em.dsem("dbg")
    r_dbg = R("dbg")

    def dump(name, ap, shape, dt, reads):
        if not debug:
            return
        t = nc.dram_tensor("dbg_" + name, list(shape), dt, kind="ExternalOutput")
        em.dma("sp", s_dbg, t.ap(), ap, reads=reads, writes=[r_dbg])

    ALLE = ("sp", "pe", "act", "dve", "pool")

    def barrier(regions):
        for e_ in ALLE:
            em.wait_all(e_, regions)

    with ExitStack() as top:
        sb = lambda name, shape, dt=F32: top.enter_context(nc.sbuf_tensor(name, list(shape), dt))
        modT = sb("modT", [128, 96, 2])
        r_modT = R("modT")
        A1 = sb("A1", [128, KC, 2])
        r_A1 = R("A1")
        A2 = sb("A2", [128, KC])
        r_A2 = R("A2")
        ones_bf = sb("ones_bf", [128, 128], BF16)
        r_ones = R("ones")
        I("pool", "memset", [], [r_ones], ones_bf[:], 1.0)
        cst = sb("cst", [128, NCONST])
        ppt = sb("ppt", [128, NPP])
        r_cst = R("cst")
        s_cst = em.dsem("cst")
        em.dma("sp", s_cst, cst[:], consts[:, :], writes=[r_cst])
        em.dma("sp", s_cst, ppt[:], pp[:, :], writes=[r_cst])
        ident_bf = sb("ident_bf", [128, 128], BF16)
        I("dve", "tensor_copy", [r_cst], [r_ones], out=ident_bf[:], in_=cst[:, C_ID:C_ID + 128])
        ident_f = cst[:, C_ID:C_ID + 128]

        with ExitStack() as ph:
            psb = lambda name, shape, dt=F32: ph.enter_context(nc.sbuf_tensor(name, list(shape), dt))
            csb = psb("csb", [128, KC, 2])
            r_csb = R()
            s_c = em.dsem("c")
            em.dma("sp", s_c, csb[:], c_col[:, :, :], writes=[r_csb])
            sc = psb("sc", [128, KC, 2])
            r_sc = R()
            abT = psb("abT", [128, 96])
            em.dma("sp", s_c, abT[:], ada_bT[:, :], writes=[r_csb])
            g1n = psb("g1n", [128, KC])
            em.dma("sp", s_c, g1n[:], n1g[:, :], writes=[r_csb])
            I("act", "activation", [r_csb], [r_sc], out=sc[:], in_=csb[:], func=ACTF.Silu)
            blks = [(psb("adablk%d" % i, [128, KC, 512]), R(), em.dsem("ada%d" % i)) for i in range(2)]
            rot = Rot(blks)
            ps_mod = ph.enter_context(nc.psum_tensor("ps_mod", [128, 96, 2], F32))
            r_psmod = R()
            ada_v = ada_w.ap().rearrange("(kc p) n -> p kc n", p=128)
            for cb in range(DBG.get('p0', 24)):
                t, rg, sm = rot.next()
                for half in range(2):
                    em.dma("sp", sm, t[:, half * 8:(half + 1) * 8, :],
                           ada_v[:, half * 8:(half + 1) * 8, cb * 512:(cb + 1) * 512], writes=[rg])
                for j4 in range(4):
                    j = cb * 4 + j4
                    for kc in range(KC):
                        I("pe", "matmul", [rg, r_sc], [r_psmod], ps_mod[:, j, :],
                          t[:, kc, j4 * 128:(j4 + 1) * 128], sc[:, kc, :], start=(kc == 0), stop=(kc == KC - 1))
            I("dve", "tensor_tensor", [r_psmod, r_csb], [r_modT], out=modT[:], in0=ps_mod[:],
              in1=abT[:].unsqueeze(2).broadcast_to([128, 96, 2]), op=ALU.add)
            I("dve", "tensor_scalar", [r_modT], [r_A1], out=A1[:], in0=modT[:, 16:32, :], scalar1=1.0,
              scalar2=None, op0=ALU.add)
            I("dve", "tensor_tensor", [r_A1, r_csb], [r_A1], out=A1[:], in0=A1[:],
              in1=g1n[:].unsqueeze(2).broadcast_to([128, KC, 2]), op=ALU.mult)
            I("dve", "tensor_scalar", [r_modT], [r_A2], out=A2[:], in0=modT[:, 64:80, 0], scalar1=1.0,
              scalar2=None, op0=ALU.add)
            I("dve", "tensor_tensor", [r_A2, r_cst], [r_A2], out=A2[:], in0=A2[:], in1=ppt[:, P_N2:P_N2 + 16],
              op=ALU.mult)
            barrier([r_A1, r_A2, r_modT, r_psmod, r_sc, r_csb] + [b[1] for b in blks])

        with ExitStack() as ph:
            psb = lambda name, shape, dt=F32: ph.enter_context(nc.sbuf_tensor(name, list(shape), dt))
            Wb = psb("Wb", [128, KC, NCOL], BF16)
            r_Wb = R()
            Wc_v = Wc.ap().rearrange("(kc p) n -> p kc n", p=128)
            wst = [(psb("wst%d" % i, [128, NCOL]), R(), em.dsem("wst%d" % i)) for i in range(2)]
            wrot = Rot(wst)
            for kc in range(KC):
                w_t, r_w, s_wst = wrot.next()
                em.dma("sp", s_wst, w_t[:], Wc_v[:, kc, :], writes=[r_w])
                I("pool", "tensor_copy", [r_w], [r_Wb], out=Wb[:, kc, :], in_=w_t[:])
            xs = [(psb("xt%d" % i, [128, KC, 512]), R(), em.dsem("xt%d" % i)) for i in range(2)]
            xrot = Rot(xs)
            sq = psb("sq", [128, KC, 512], BF16)
            r_sq = R()
            hT = psb("hT", [128, KC, 512], BF16)
            r_hT = R()
            xn, r_xn = sq, r_sq
            rstd = psb("rstd", [128, 512])
            r_rstd = R()
            stA = [(psb("stA%d" % i, [128, 512], BF16), R(), em.dsem("stA%d" % i)) for i in range(3)]
            stR = [(psb("stR%d" % i, [128, 512], F32), R(), em.dsem("stR%d" % i)) for i in range(3)]
            rotA, rotR = Rot(stA), Rot(stR)
            ps_ss = ph.enter_context(nc.psum_tensor("ps_ss", [128, 512], F32))
            r_psss = R()
            pss = [(ph.enter_context(nc.psum_tensor("ps_z%d" % i, [128, 512], F32)), R()) for i in range(4)]
            psrot = Rot(pss)
            xT_v = xT.ap().rearrange("(kc p) t -> p kc t", p=128)
            tiles = [(0, 256, 1)] + [(256 + i * 512, 512, 0) for i in range(16)]
            ctiles = [(i * 128, 128) for i in range(5)] + [(640, 64)] + \
                     [(NA + i * 128, 128) for i in range(9)] + [(NA + 1152, 32)]
            evac_i = 0
            for (t0, N, s) in tiles[:DBG.get('p1', 17)]:
                xt, r_xt, s_xt = xrot.next()
                for half in range(2):
                    em.dma("sp", s_xt, xt[:, half * 8:(half + 1) * 8, 0:N],
                           xT_v[:, half * 8:(half + 1) * 8, t0:t0 + N], writes=[r_xt])
                I("act", "activation", [r_xt], [r_sq], out=sq[:, :, 0:N], in_=xt[:, :, 0:N], func=ACTF.Square)
                for kc in range(KC):
                    I("pe", "matmul", [r_sq, r_ones], [r_psss], ps_ss[:, 0:N], ones_bf[:], sq[:, kc, 0:N],
                      start=(kc == 0), stop=(kc == KC - 1))
                I("dve", "tensor_scalar", [r_psss], [r_rstd], out=rstd[:, 0:N], in0=ps_ss[:, 0:N],
                  scalar1=1.0 / D, scalar2=1e-6, op0=ALU.mult, op1=ALU.add)
                I("act", "activation", [r_rstd], [r_rstd], out=rstd[:, 0:N], in_=rstd[:, 0:N], func=ACTF.Sqrt)
                I("dve", "reciprocal", [r_rstd], [r_rstd], out=rstd[:, 0:N], in_=rstd[:, 0:N])
                I("dve", "tensor_tensor", [r_xt, r_rstd], [r_xn], out=xn[:, :, 0:N], in0=xt[:, :, 0:N],
                  in1=rstd[:, 0:N].unsqueeze(1).broadcast_to([128, KC, N]), op=ALU.mult)
                for kc in range(KC):
                    if kc % 2 == 0:
                        I("act", "activation", [r_xn, r_A1, r_modT], [r_hT], out=hT[:, kc, 0:N], in_=xn[:, kc, 0:N],
                          func=ACTF.Identity, scale=A1[:, kc, s:s + 1], bias=modT[:, kc, s:s + 1])
                    else:
                        I("pool", "tensor_scalar", [r_xn, r_A1, r_modT], [r_hT], out=hT[:, kc, 0:N],
                          in0=xn[:, kc, 0:N], scalar1=A1[:, kc, s:s + 1], scalar2=modT[:, kc, s:s + 1],
                          op0=ALU.mult, op1=ALU.add)
                for (c0, M) in ctiles:
                    ps, r_ps = psrot.next()
                    for kc in range(KC):
                        I("pe", "matmul", [r_Wb, r_hT], [r_ps], ps[0:M, 0:N], Wb[:, kc, c0:c0 + M], hT[:, kc, 0:N],
                          start=(kc == 0), stop=(kc == KC - 1))
                    isA = c0 < NA
                    st, r_st, s_st = (rotA if isA else rotR).next()
                    if evac_i % 2 == 0:
                        I("act", "activation", [r_ps], [r_st], out=st[0:M, 0:N], in_=ps[0:M, 0:N], func=ACTF.Copy)
                    else:
                        I("dve", "tensor_copy", [r_ps], [r_st], out=st[0:M, 0:N], in_=ps[0:M, 0:N])
                    evac_i += 1
                    if isA:
                        em.dma("sp", s_st, zA[c0:c0 + M, t0:t0 + N], st[0:M, 0:N], reads=[r_st], writes=[r_zA])
                    else:
                        em.dma("sp", s_st, zR[c0 - NA:c0 - NA + M, t0:t0 + N], st[0:M, 0:N],
                               reads=[r_st], writes=[r_zR])
            barrier([r_zA, r_zR, r_hT, r_xn, r_sq, r_Wb] + [x[1] for x in stA + stR + pss + xs + wst])

        if stop_after >= 2:
          with ExitStack() as ph:
            psb = lambda name, shape, dt=F32: ph.enter_context(nc.sbuf_tensor(name, list(shape), dt))
            pst = lambda name, shape, dt=F32: ph.enter_context(nc.psum_tensor(name, list(shape), dt))
            KT = psb("KT", [64, TT + 128], BF16)
            r_KT = R()
            Vtm = psb("Vtm", [128, 67, 64], BF16)
            r_Vtm = R()
            QT = psb("QT", [64, 4, SEQ], BF16)
            r_QT = R()
            vT = psb("vT", [64, TT], BF16)
            r_vT = R()
            I("pool", "memset", [], [r_KT], KT[:, TT:TT + 128], 0.0)
            I("pool", "memset", [], [r_Vtm], Vtm[:, 66, :], 0.0)
            s_a0 = em.dsem("a0")
            if not DBG.get("skipA"):
                em.dma("sp", s_a0, KT[:, 0:CTX], zA[512:576, 0:CTX], reads=[r_zA], writes=[r_KT])
                em.dma("sp", s_a0, vT[:], zA[640:704, :], reads=[r_zA], writes=[r_vT])
            psV = pst("psV", [128, 8, 64], BF16)
            r_psV = R()
            for b0 in range(0, DBG.get('p2v', 66), 8):
                nb = min(8, 66 - b0)
                for j in range(nb):
                    I("pe", "transpose", [r_vT, r_ones], [r_psV], psV[:, j, :], vT[:, (b0 + j) * 128:(b0 + j + 1) * 128],
                      ident_bf[0:64, 0:64])
                I("dve", "tensor_copy", [r_psV], [r_Vtm], out=Vtm[:, b0:b0 + nb, :], in_=psV[:, 0:nb, :])
            cs = [(psb("cs%d" % i, [64, 2, 512]), R(), em.dsem("cs%d" % i)) for i in range(2)]
            csrot = Rot(cs)
            zz = [(psb("zz%d" % i, [64, 2, 512], BF16), R(), em.dsem("zz%d" % i)) for i in range(3)]
            zrot = Rot(zz)
            t1 = psb("t1", [64, 512])
            t2 = psb("t2", [64, 512])
            r_t1, r_t2 = R(), R()
            for i in range(DBG.get('p2r', 16)):
                c_t, r_c, s_cc = csrot.next()
                em.dma("sp", s_cc, c_t[:], rope[:, :, i * 512:(i + 1) * 512], writes=[r_c])
                for src in range(5):
                    z_t, r_z, s_z = zrot.next()
                    row = src * 64 if src < 4 else 512
                    prow = 256 + src * 64 if src < 4 else 576
                    em.dma("sp", s_z, z_t[:, 0, :], zA[row:row + 64, CTX + i * 512:CTX + (i + 1) * 512],
                           reads=[r_zA], writes=[r_z])
                    em.dma("sp", s_z, z_t[:, 1, :], zA[prow:prow + 64, CTX + i * 512:CTX + (i + 1) * 512],
                           reads=[r_zA], writes=[r_z])
                    I("dve", "tensor_tensor", [r_z, r_c], [r_t1], out=t1[:], in0=z_t[:, 0, :], in1=c_t[:, 0, :],
                      op=ALU.mult)
                    I("pool", "tensor_tensor", [r_z, r_c], [r_t2], out=t2[:], in0=z_t[:, 1, :], in1=c_t[:, 1, :],
                      op=ALU.mult)
                    if src < 4:
                        I("dve", "tensor_tensor", [r_t1, r_t2], [r_QT], out=QT[:, src, i * 512:(i + 1) * 512],
                          in0=t1[:], in1=t2[:], op=ALU.add)
                    else:
                        I("dve", "tensor_tensor", [r_t1, r_t2], [r_KT],
                          out=KT[:, CTX + i * 512:CTX + (i + 1) * 512], in0=t1[:], in1=t2[:], op=ALU.add)
            sA = [(pst("sA%d" % i, [128, 512]), R()) for i in range(2)]
            sB = [(pst("sB%d" % i, [128, 128]), R()) for i in range(2)]
            pTa = pst("pTa", [128, 4, 128])
            pTb = pst("pTb", [128, 128])
            r_pTa, r_pTb = R(), R()
            psO = pst("psO", [64, 128])
            r_psO = R()
            pr = [(psb("pA%d" % i, [128, 640], BF16), R()) for i in range(2)]
            pT_sb = [(psb("pTs%d" % i, [128, 5, 128], BF16), R()) for i in range(2)]
            sm = [(psb("sm%d" % i, [128, 8]), R()) for i in range(2)]
            dg = [(psb("dg%d" % i, [128, 128], BF16), R()) for i in range(2)]
            ost = [(psb("ost%d" % i, [64, 512], BF16), R(), em.dsem("ost%d" % i)) for i in range(2)]
            def attn_gen(hh, blk, par):
                o_t, r_o, s_o = ost[(hh * 16 + blk // 4) % 2]
                sa, r_sa = sA[par]
                sb_, r_sb = sB[par]
                p_t, r_p = pr[par]
                pt_t, r_pt = pT_sb[par]
                m_t, r_m = sm[par]
                d_t, r_d = dg[par]
                q_ap = QT[:, hh, blk * 128:(blk + 1) * 128]
                kc0 = CTX + (blk - 1) * 128
                I("pe", "matmul", [r_QT, r_KT], [r_sa], sa[:, 0:256], q_ap, KT[:, 0:256], start=True, stop=True)
                I("pe", "matmul", [r_QT, r_KT], [r_sa], sa[:, 256:512], q_ap, KT[:, kc0:kc0 + 256],
                  start=True, stop=True)
                I("pe", "matmul", [r_QT, r_KT], [r_sb], sb_[:, :], q_ap, KT[:, kc0 + 256:kc0 + 384],
                  start=True, stop=True)
                yield
                mp_c = C_MF if blk == 0 else C_MP
                mn_c = C_MF if blk == 63 else C_MN
                I("dve", "tensor_tensor", [r_sa, r_cst], [r_sa], out=sa[:, 256:384], in0=sa[:, 256:384],
                  in1=cst[:, mp_c:mp_c + 128], op=ALU.add)
                I("dve", "tensor_tensor", [r_sb, r_cst], [r_sb], out=sb_[:, :], in0=sb_[:, :],
                  in1=cst[:, mn_c:mn_c + 128], op=ALU.add)
                I("dve", "reduce_max", [r_sa], [r_m], out=m_t[:, 0:1], in_=sa[:, :], axis=AX.X)
                I("dve", "reduce_max", [r_sb], [r_m], out=m_t[:, 1:2], in_=sb_[:, :], axis=AX.X)
                I("dve", "tensor_tensor", [r_m], [r_m], out=m_t[:, 2:3], in0=m_t[:, 0:1], in1=m_t[:, 1:2],
                  op=ALU.max)
                I("dve", "tensor_scalar", [r_m, r_cst], [r_m], out=m_t[:, 3:4], in0=m_t[:, 2:3], scalar1=0.125,
                  scalar2=ppt[:, P_SINK + hh:P_SINK + hh + 1], op0=ALU.mult, op1=ALU.max)
                I("dve", "tensor_scalar", [r_m], [r_m], out=m_t[:, 4:5], in0=m_t[:, 3:4], scalar1=-1.0,
                  scalar2=None, op0=ALU.mult)
                yield
                I("act", "activation", [r_sa, r_m], [r_p, r_m], out=p_t[:, 0:512], in_=sa[:, :], func=ACTF.Exp,
                  scale=0.125, bias=m_t[:, 4:5], accum_out=m_t[:, 5:6])
                I("act", "activation", [r_sb, r_m], [r_p, r_m], out=p_t[:, 512:640], in_=sb_[:, :], func=ACTF.Exp,
                  scale=0.125, bias=m_t[:, 4:5], accum_out=m_t[:, 6:7])
                I("act", "activation", [r_m, r_cst], [r_m], out=m_t[:, 7:8], in_=m_t[:, 4:5], func=ACTF.Exp,
                  scale=1.0, bias=ppt[:, P_SINK + hh:P_SINK + hh + 1])
                yield
                I("dve", "tensor_tensor", [r_m], [r_m], out=m_t[:, 5:6], in0=m_t[:, 5:6], in1=m_t[:, 6:7],
                  op=ALU.add)
                I("dve", "tensor_tensor", [r_m], [r_m], out=m_t[:, 5:6], in0=m_t[:, 5:6], in1=m_t[:, 7:8],
                  op=ALU.add)
                I("dve", "reciprocal", [r_m], [r_m], out=m_t[:, 6:7], in_=m_t[:, 5:6])
                I("dve", "tensor_scalar", [r_m, r_cst], [r_d], out=d_t[:], in0=cst[:, C_ID:C_ID + 128],
                  scalar1=m_t[:, 6:7], scalar2=None, op0=ALU.mult)
                yield
                for kb in range(4):
                    I("pe", "matmul", [r_p, r_d], [r_pTa], pTa[:, kb, :], p_t[:, kb * 128:(kb + 1) * 128], d_t[:],
                      start=True, stop=True)
                I("pe", "matmul", [r_p, r_d], [r_pTb], pTb[:, :], p_t[:, 512:640], d_t[:], start=True, stop=True)
                I("act", "activation", [r_pTa], [r_pt], out=pt_t[:, 0:4, :], in_=pTa[:, :, :], func=ACTF.Copy)
                I("dve", "tensor_copy", [r_pTb], [r_pt], out=pt_t[:, 4, :], in_=pTb[:, :])
                yield
                vblk = [0, 1, 2 + blk - 1 if blk > 0 else 0, 2 + blk, 2 + blk + 1]
                for kb in range(5):
                    I("pe", "matmul", [r_Vtm, r_pt], [r_psO], psO[:, :], Vtm[:, vblk[kb], :], pt_t[:, kb, :],
                      start=(kb == 0), stop=(kb == 4))
                I("dve", "tensor_copy", [r_psO], [r_o], out=o_t[:, (blk % 4) * 128:(blk % 4 + 1) * 128],
                  in_=psO[:, :])
                if blk % 4 == 3:
                    tok0 = (blk // 4) * 512
                    qd, col = tok0 // TOK, tok0 % TOK
                    em.dma("sp", s_o, mixL[qd * 512 + hh * 64:qd * 512 + hh * 64 + 64, col:col + 512], o_t[:],
                           reads=[r_o], writes=[r_mixL])
                yield

            for hh in range(DBG.get('p2h', 4)):
                for blk0 in range(0, 64, 2):
                    live = [attn_gen(hh, blk0 + j, j) for j in range(2)]
                    while live:
                        for g_ in list(live):
                            try:
                                next(g_)
                            except StopIteration:
                                live.remove(g_)
            barrier([r_mixL, r_KT, r_QT, r_Vtm, r_vT, r_psV, r_psO, r_pTa, r_pTb, r_t1, r_t2] +
                    [x[1] for x in sA + sB + pr + pT_sb + sm + dg + ost + cs + zz])

        if stop_after >= 3:
          with ExitStack() as ph:
            psb = lambda name, shape, dt=F32: ph.enter_context(nc.sbuf_tensor(name, list(shape), dt))
            pst = lambda name, shape, dt=F32: ph.enter_context(nc.psum_tensor(name, list(shape), dt))
            lo32 = psb("lo32", [64, 2, 2, 4, 64])
            lob = psb("lob", [64, 2, 2, 4, 64], BF16)
            gu32 = psb("gu32", [128, 2, 256])
            gub = psb("gub", [128, 2, 256], BF16)
            r_par = R()
            s_par = em.dsem("par")
            em.dma("sp", s_par, lo32[:], lora[:, :, :, :, :], writes=[r_par])
            em.dma("sp", s_par, gu32[:, 0, :], gup[0:128, :], writes=[r_par])
            em.dma("sp", s_par, gu32[0:32, 1, :], gup[128:160, :], writes=[r_par])
            r_lob = R()
            I("pool", "memset", [], [r_lob], gub[:, 1, :], 0.0)
            I("dve", "tensor_copy", [r_par], [r_lob], out=lob[:], in_=lo32[:])
            I("dve", "tensor_copy", [r_par], [r_lob], out=gub[:, 0, :], in_=gu32[:, 0, :])
            I("dve", "tensor_copy", [r_par], [r_lob], out=gub[0:32, 1, :], in_=gu32[0:32, 1, :])
            tsc = psb("tsc", [128, 18])
            I("dve", "tensor_tensor", [r_cst], [r_lob], out=tsc[:], in0=ppt[:, P_TS:P_TS + 18],
              in1=ppt[:, P_TS + 18:P_TS + 36], op=ALU.add)
            I("dve", "tensor_scalar", [r_lob], [r_lob], out=tsc[:], in0=tsc[:], scalar1=-1.0, scalar2=1.0,
              op0=ALU.mult, op1=ALU.add)
            omka = psb("omka", [64, 4])
            I("dve", "tensor_scalar", [r_cst], [r_lob], out=omka[:], in0=ppt[0:64, P_HP + 20:P_HP + 24], scalar1=-1.0,
              scalar2=1.0, op0=ALU.mult, op1=ALU.add)
            hp = lambda h, i: ppt[0:64, P_HP + i * 4 + h:P_HP + i * 4 + h + 1]
            ones64 = ones_bf[0:64, 0:64]
            BKbd = [(psb("BKbd%d" % i, [128, 2, 128], BF16), R()) for i in range(2)]
            for t_, r_ in BKbd:
                I("pool", "memset", [], [r_], t_[:], 0.0)
            zl = [(psb("zl%d" % i, [128, 514]), R(), em.dsem("zl%d" % i)) for i in range(3)]
            zlrot = Rot(zl)
            twd = psb("twd", [64, 2, 512], BF16)
            adz = psb("adz", [64, 2, 512], BF16)
            sgb = psb("sgb", [128, 2, 512], BF16)
            r_twd, r_adz, r_sgb = R(), R(), R()
            zsh = psb("zsh", [128, 512])
            r_zsh = R()
            I("pool", "memset", [], [r_sgb], sgb[:, 1, :], 0.0)
            idpad = psb("idpad", [64, 128], BF16)
            I("pool", "memset", [], [r_lob], idpad[:], 0.0)
            I("dve", "tensor_copy", [r_cst, r_lob], [r_lob], out=idpad[:, 0:64], in_=cst[0:64, C_ID:C_ID + 64])
            rr = psb("rr", [64, 512]); kk_ = psb("kk_", [64, 512]); vv = psb("vv", [64, 512])
            r_rr, r_kk, r_vv = R(), R(), R()
            sg = psb("sg", [64, 2, 512]); av = psb("av", [64, 2, 512]); kd = psb("kd", [64, 2, 512])
            bv = psb("bv", [64, 2, 512]); Pc = psb("Pc", [64, 2, 512])
            r_sg, r_av, r_kd, r_bv, r_Pc = R(), R(), R(), R(), R()
            kkn = psb("kkn", [64, 512]); nkk = psb("nkk", [64, 512]); tmpa = psb("tmpa", [64, 512])
            tmpb = psb("tmpb", [64, 512]); kq = psb("kq", [64, 512], BF16)
            r_kkn, r_nkk, r_tmpa, r_tmpb, r_kq = R(), R(), R(), R(), R()
            Ein = psb("Ein", [64, 2, 512]); Eng = psb("Eng", [64, 2, 512]); Eex = psb("Eex", [64, 2, 512])
            EC = psb("EC", [64, 2, 512])
            r_Ein, r_Eng, r_Eex, r_EC = R(), R(), R(), R()
            opsets = []
            for i_ in range(2):
                opsets.append((psb("AR%d" % i_, [64, 8, 2, 2, 64], BF16), psb("Bt%d" % i_, [64, 8, 2, 64], BF16),
                               psb("Kt%d" % i_, [64, 8, 2, 64], BF16), psb("Bh%d" % i_, [64, 8, 2, 64], BF16),
                               psb("Kh%d" % i_, [64, 8, 2, 64], BF16), psb("Vd%d" % i_, [64, 8, 2, 64], BF16),
                               psb("RK%d" % i_, [64, 8, 64], BF16), psb("gam%d" % i_, [64, 2, 8]),
                               R(), R(), R(), R(), R(), R(), R(), R()))
            REC = [None]
            psX = [(pst("psX%d" % i, [64, 512]), R()) for i in range(2)]
            pxrot = Rot(psX)
            psT = pst("psT", [128, 4, 64], BF16); r_psT = R()
            psG = pst("psG", [128, 512]); r_psG = R()
            psL = pst("psL", [128, 2, 128]); r_psL = R()
            psY = [(pst("psY%d" % i, [128, 128]), R()) for i in range(2)]
            ps45 = pst("ps45", [128, 512]); r_ps45 = R()
            Vtm2 = [(psb("Vtm2_%d" % i, [128, 128], BF16), R()) for i in range(2)]
            for t_, r_ in Vtm2:
                I("pool", "memset", [], [r_], t_[:], 0.0)
            GBK = [(psb("GBK%d" % i, [128, 512], BF16), R()) for i in range(2)]
            NL = [(psb("NL%d" % i, [128, 2, 128], BF16), R()) for i in range(6)]
            nlrots = [Rot(NL[0:3]), Rot(NL[3:6])]
            Yc = [(psb("Yc%d" % i, [128, 128], BF16), R()) for i in range(4)]
            ycrots = [Rot(Yc[0:2]), Rot(Yc[2:4])]
            Wsb = [(psb("Wsb%d" % i, [128, 128], BF16), R()) for i in range(2)]
            oPA = [(psb("oPA%d" % i, [64, 256]), R(), em.dsem("oPA%d" % i)) for i in range(2)]
            oQB = [(psb("oQB%d" % i, [128, 128]), R(), em.dsem("oQB%d" % i)) for i in range(2)]
            oBG = [(psb("oBG%d" % i, [64, 128]), R(), em.dsem("oBG%d" % i)) for i in range(2)]
            I0 = I

            def I(eng, method, r, w, *args, **kw):
                if REC[0] is not None:
                    REC[0].append(("I", (eng, method, r, w) + args, kw))
                else:
                    I0(eng, method, r, w, *args, **kw)

            def DM(q, semk, out, in_, reads=(), writes=()):
                if REC[0] is not None:
                    REC[0].append(("D", (q, semk, out, in_), dict(reads=reads, writes=writes)))
                else:
                    em.dma(q, semk, out, in_, reads=reads, writes=writes)

            def replay(ops):
                for kind, a, kw in ops:
                    if kind == "I":
                        I0(*a, **kw)
                    else:
                        em.dma(*a, **kw)

            unit_i = 0
            c3 = lambda ap, n: ap.rearrange("p (c t) -> p c t", t=64)

            def shifted_load(row0, nrows, grp, t0, N, first, last):
                z_t, r_z, s_z = zlrot.next()
                lo = 0 if first else 1
                hi = 0 if last else 1
                if first:
                    I("pool", "memset", [], [r_z], z_t[0:nrows, 0:1], 0.0)
                if last:
                    I("pool", "memset", [], [r_z], z_t[0:nrows, N + 1:N + 2], 0.0)
                DM("sp", s_z, z_t[0:nrows, 1 - lo:N + 1 + hi], zR[row0:row0 + nrows, t0 - lo:t0 + N + hi],
                  reads=[r_zR], writes=[r_z])
                I("dve", "tensor_scalar", [r_z, r_lob], [r_zsh], out=zsh[0:nrows, 0:N], in0=z_t[0:nrows, 1:N + 1],
                  scalar1=tsc[0:nrows, grp:grp + 1], scalar2=None, op0=ALU.mult)
                I("dve", "scalar_tensor_tensor", [r_z, r_cst, r_zsh], [r_zsh], out=zsh[0:nrows, 0:N],
                  in0=z_t[0:nrows, 0:N], scalar=ppt[0:nrows, P_TS + grp:P_TS + grp + 1], in1=zsh[0:nrows, 0:N],
                  op0=ALU.mult, op1=ALU.add)
                I("dve", "scalar_tensor_tensor", [r_z, r_cst, r_zsh], [r_zsh], out=zsh[0:nrows, 0:N],
                  in0=z_t[0:nrows, 2:N + 2], scalar=ppt[0:nrows, P_TS + 18 + grp:P_TS + 18 + grp + 1],
                  in1=zsh[0:nrows, 0:N], op0=ALU.mult, op1=ALU.add)

            tiles = [(0, 256)] + [(256 + i * 512, 512) for i in range(16)]
            for (t0, N) in tiles[:DBG.get('p3t', 17)]:
                first = t0 in (0, CTX)
                last = (t0 + N) in (CTX, TT)
                nc_ = N // 64
                ch0 = t0 // 64
                for d in range(2):
                    shifted_load(768 + d * 64, 64, 12 + d, t0, N, first, last)
                    I("act", "activation", [r_zsh], [r_twd], out=twd[:, d, 0:N], in_=zsh[0:64, 0:N], func=ACTF.Tanh)
                for d in range(2):
                    shifted_load(896 + d * 64, 64, 14 + d, t0, N, first, last)
                    I("act", "activation", [r_zsh], [r_adz], out=adz[:, d, 0:N], in_=zsh[0:64, 0:N], func=ACTF.Copy)
                shifted_load(1024, 128, 16, t0, N, first, last)
                I("act", "activation", [r_zsh], [r_sgb], out=sgb[:, 0, 0:N], in_=zsh[:, 0:N], func=ACTF.Sigmoid)
                shifted_load(1152, 32, 17, t0, N, first, last)
                I("act", "activation", [r_zsh], [r_sgb], out=sgb[0:32, 1, 0:N], in_=zsh[0:32, 0:N], func=ACTF.Sigmoid)
                def prep(h, par):
                    (AR, Bt, Kt, Bh, Kh, Vd, RK, gam, r_AR, r_Bt, r_Kt, r_Bh, r_Kh, r_Vd, r_RK, r_gam) = opsets[par]
                    shifted_load(h * 64, 64, h, t0, N, first, last)
                    I("act", "activation", [r_zsh], [r_rr], out=rr[:, 0:N], in_=zsh[0:64, 0:N], func=ACTF.Copy)
                    shifted_load(256 + h * 64, 64, 4 + h, t0, N, first, last)
                    I("act", "activation", [r_zsh], [r_kk], out=kk_[:, 0:N], in_=zsh[0:64, 0:N], func=ACTF.Copy)
                    shifted_load(512 + h * 64, 64, 8 + h, t0, N, first, last)
                    I("act", "activation", [r_zsh], [r_vv], out=vv[:, 0:N], in_=zsh[0:64, 0:N], func=ACTF.Copy)
                    for d in range(2):
                        px, r_px = pxrot.next()
                        I("pe", "matmul", [r_lob, r_twd], [r_px], px[:, 0:N], lob[:, 0, d, h, :], twd[:, d, 0:N],
                          start=True, stop=True)
                        I("act", "activation", [r_px, r_cst], [r_sg], out=sg[:, d, 0:N], in_=px[:, 0:N],
                          func=ACTF.Sigmoid, bias=hp(h, d), scale=1.0)
                        px, r_px = pxrot.next()
                        I("pe", "matmul", [r_lob, r_adz], [r_px], px[:, 0:N], lob[:, 1, d, h, :], adz[:, d, 0:N],
                          start=True, stop=True)
                        I("act", "activation", [r_px, r_cst], [r_av], out=av[:, d, 0:N], in_=px[:, 0:N],
                          func=ACTF.Sigmoid, bias=hp(h, 2 + d), scale=1.0)
                    I("dve", "tensor_scalar", [r_kk, r_cst], [r_kkn], out=kkn[:, 0:N], in0=kk_[:, 0:N], scalar1=hp(h, 4),
                      scalar2=None, op0=ALU.mult)
                    I("pool", "tensor_tensor", [r_kkn], [r_kq], out=kq[:, 0:N], in0=kkn[:, 0:N], in1=kkn[:, 0:N],
                      op=ALU.mult)
                    px, r_px = pxrot.next()
                    I("pe", "matmul", [r_ones, r_kq], [r_px], px[:, 0:N], ones64, kq[:, 0:N], start=True, stop=True)
                    I("dve", "tensor_scalar", [r_px], [r_tmpa], out=tmpa[:, 0:N], in0=px[:, 0:N], scalar1=1e-12,
                      scalar2=None, op0=ALU.max)
                    I("act", "activation", [r_tmpa], [r_tmpa], out=tmpa[:, 0:N], in_=tmpa[:, 0:N], func=ACTF.Sqrt)
                    I("dve", "reciprocal", [r_tmpa], [r_tmpa], out=tmpa[:, 0:N], in_=tmpa[:, 0:N])
                    I("dve", "tensor_tensor", [r_kkn, r_tmpa], [r_kkn], out=kkn[:, 0:N], in0=kkn[:, 0:N],
                      in1=tmpa[:, 0:N], op=ALU.mult)
                    I("pool", "tensor_scalar", [r_kkn], [r_nkk], out=nkk[:, 0:N], in0=kkn[:, 0:N], scalar1=-1.0,
                      scalar2=None, op0=ALU.mult)
                    for d in range(2):
                        I("dve", "tensor_scalar", [r_av, r_cst, r_lob], [r_tmpb], out=tmpb[:, 0:N], in0=av[:, d, 0:N],
                          scalar1=hp(h, 5), scalar2=omka[:, h:h + 1], op0=ALU.mult, op1=ALU.add)
                        I("dve", "tensor_tensor", [r_tmpb, r_kk], [r_kd], out=kd[:, d, 0:N], in0=tmpb[:, 0:N],
                          in1=kk_[:, 0:N], op=ALU.mult)
                        I("pool", "tensor_tensor", [r_kkn, r_av], [r_bv], out=bv[:, d, 0:N], in0=kkn[:, 0:N],
                          in1=av[:, d, 0:N], op=ALU.mult)
                        I("dve", "tensor_tensor_scan", [r_sg, r_cst], [r_Pc], out=Pc[:, d, 0:N],
                          data0=cst[0:64, C_RS:C_RS + N], data1=sg[:, d, 0:N], initial=0.0, op0=ALU.mult, op1=ALU.add)
                    P3 = lambda d: c3(Pc[:, d, 0:N], nc_)
                    tot = lambda d: P3(d)[:, :, 63:64]
                    totb = lambda d: tot(d).broadcast_to([64, nc_, 64])
                    v3 = lambda ap: c3(ap, nc_)
                    I("act", "activation", [r_Pc], [r_Ein], out=Ein[:, 0, 0:N], in_=Pc[:, 0, 0:N], func=ACTF.Exp, scale=CNEG)
                    I("act", "activation", [r_Pc], [r_Eng], out=Eng[:, 0, 0:N], in_=Pc[:, 0, 0:N], func=ACTF.Exp, scale=-CNEG)
                    I("dve", "tensor_tensor", [r_Pc, r_sg], [r_tmpa], out=tmpa[:, 0:N], in0=Pc[:, 0, 0:N], in1=sg[:, 0, 0:N],
                      op=ALU.subtract)
                    I("act", "activation", [r_tmpa], [r_Eex], out=Eex[:, 0, 0:N], in_=tmpa[:, 0:N], func=ACTF.Exp, scale=CNEG)
                    I("dve", "tensor_tensor", [r_Pc], [r_tmpb], out=v3(tmpb[:, 0:N]), in0=totb(0), in1=P3(0),
                      op=ALU.subtract)
                    I("act", "activation", [r_tmpb], [r_EC], out=EC[:, 0, 0:N], in_=tmpb[:, 0:N], func=ACTF.Exp, scale=CNEG)
                    I("dve", "tensor_tensor", [r_Pc], [r_tmpa], out=v3(tmpa[:, 0:N]), in0=totb(1), in1=P3(1),
                      op=ALU.subtract)
                    I("act", "activation", [r_tmpa], [r_Eex], out=Eex[:, 1, 0:N], in_=tmpa[:, 0:N], func=ACTF.Exp, scale=CNEG)
                    I("dve", "tensor_tensor", [r_tmpa, r_sg], [r_tmpa], out=tmpa[:, 0:N], in0=tmpa[:, 0:N], in1=sg[:, 1, 0:N],
                      op=ALU.add)
                    I("act", "activation", [r_tmpa], [r_Ein], out=Ein[:, 1, 0:N], in_=tmpa[:, 0:N], func=ACTF.Exp, scale=CNEG)
                    I("act", "activation", [r_tmpa], [r_Eng], out=Eng[:, 1, 0:N], in_=tmpa[:, 0:N], func=ACTF.Exp, scale=-CNEG)
                    I("dve", "tensor_tensor", [r_Pc, r_sg], [r_tmpb], out=tmpb[:, 0:N], in0=Pc[:, 1, 0:N], in1=sg[:, 1, 0:N],
                      op=ALU.subtract)
                    I("act", "activation", [r_tmpb], [r_EC], out=EC[:, 1, 0:N], in_=tmpb[:, 0:N], func=ACTF.Exp, scale=CNEG)
                    for d in range(2):
                        I("act", "activation", [r_Pc], [r_gam], out=gam[:, d, 0:nc_].unsqueeze(2), in_=tot(d), func=ACTF.Exp,
                          scale=CNEG)
                    for d in range(2):
                        e1, e2 = ("dve", "pool") if d == 0 else ("pool", "dve")
                        I(e1, "tensor_tensor", [r_nkk, r_Eex], [r_AR], out=AR[:, 0:nc_, 0, d, :], in0=v3(nkk[:, 0:N]),
                          in1=v3(Eex[:, d, 0:N]), op=ALU.mult)
                        I(e2, "tensor_tensor", [r_rr, r_Ein], [r_AR], out=AR[:, 0:nc_, 1, d, :], in0=v3(rr[:, 0:N]),
                          in1=v3(Ein[:, d, 0:N]), op=ALU.mult)
                        I(e1, "tensor_tensor", [r_bv, r_Eng], [r_Bt], out=Bt[:, 0:nc_, d, :], in0=v3(bv[:, d, 0:N]),
                          in1=v3(Eng[:, d, 0:N]), op=ALU.mult)
                        I(e2, "tensor_tensor", [r_kd, r_Eng], [r_Kt], out=Kt[:, 0:nc_, d, :], in0=v3(kd[:, d, 0:N]),
                          in1=v3(Eng[:, d, 0:N]), op=ALU.mult)
                        I(e1, "tensor_tensor", [r_bv, r_EC], [r_Bh], out=Bh[:, 0:nc_, d, :], in0=v3(bv[:, d, 0:N]),
                          in1=v3(EC[:, d, 0:N]), op=ALU.mult)
                        I(e2, "tensor_tensor", [r_kd, r_EC], [r_Kh], out=Kh[:, 0:nc_, d, :], in0=v3(kd[:, d, 0:N]),
                          in1=v3(EC[:, d, 0:N]), op=ALU.mult)
                        I("act", "activation", [r_vv], [r_Vd], out=Vd[:, 0:nc_, d, :], in_=v3(vv[:, 0:N]), func=ACTF.Copy)
                    I("dve", "tensor_tensor", [r_kd], [r_tmpa], out=tmpa[:, 0:N], in0=kd[:, 0, 0:N], in1=kd[:, 1, 0:N],
                      op=ALU.add)
                    I("dve", "scalar_tensor_tensor", [r_rr, r_cst, r_tmpa], [r_RK], out=RK[:, 0:nc_, :], in0=v3(rr[:, 0:N]),
                      scalar=hp(h, 6), in1=v3(tmpa[:, 0:N]), op0=ALU.mult, op1=ALU.mult)


                def unit_gen(c, par, h, hpar):
                    (AR, Bt, Kt, Bh, Kh, Vd, RK, gam, r_AR, r_Bt, r_Kt, r_Bh, r_Kh, r_Vd, r_RK, r_gam) = opsets[hpar]
                    bk_t, r_bk = BKbd[par]
                    vp_t, r_v = Vtm2[par]
                    v_t = vp_t[:, 64:128]
                    g_t, r_g = GBK[par]
                    py, r_py = psY[par]
                    w_t, r_wt = Wsb[par]
                    pa_t, r_pa, s_pa = oPA[par]
                    qb_t, r_qb, s_qb = oQB[par]
                    bg_t, r_bg, s_bg = oBG[par]
                    nlr, ycr = nlrots[par], ycrots[par]
                    fl = lambda ap: ap.rearrange("p d t -> p (d t)")
                    ARc = AR[:, c, :, :, :].rearrange("p a d t -> p (a d t)")
                    Atc = fl(AR[:, c, 0, :, :])
                    Rtc = fl(AR[:, c, 1, :, :])
                    Btc, Ktc = fl(Bt[:, c, :, :]), fl(Kt[:, c, :, :])
                    I("pe", "transpose", [r_Bh, r_ones], [r_psT], psT[:, 0, :], fl(Bh[:, c, :, :]), ident_bf[0:64, 0:64])
                    I("pe", "transpose", [r_Kh, r_ones], [r_psT], psT[:, 1, :], fl(Kh[:, c, :, :]), ident_bf[0:64, 0:64])
                    I("pe", "transpose", [r_Vd, r_ones], [r_psT], psT[:, 2, :], fl(Vd[:, c, :, :]), ident_bf[0:64, 0:64])
                    I("act", "activation", [r_psT], [r_bk], out=bk_t[0:64, :, 0:64], in_=psT[0:64, 0:2, :], func=ACTF.Copy)
                    I("act", "activation", [r_psT], [r_bk], out=bk_t[64:128, :, 64:128], in_=psT[64:128, 0:2, :],
                      func=ACTF.Copy)
                    I("act", "activation", [r_psT], [r_v], out=v_t, in_=psT[:, 2, :], func=ACTF.Copy)
                    yield
                    I("pe", "matmul", [r_Bt, r_AR], [r_psG], psG[:, 0:256], Btc, ARc, start=True, stop=True)
                    I("pe", "matmul", [r_Kt, r_AR], [r_psG], psG[:, 256:512], Ktc, ARc, start=True, stop=True)
                    I("pe", "matmul", [r_AR, r_Bt], [r_psL], psL[:, 1, :], Atc, Btc, start=True, stop=True)
                    I("dve", "tensor_tensor", [r_psG, r_cst], [r_g], out=g_t[:], in0=psG[:], in1=cst[:, C_MBK:C_MBK + 512],
                      op=ALU.mult)
                    nl, r_nl = nlr.next()
                    I("dve", "tensor_tensor", [r_psL, r_cst], [r_nl], out=nl[:, 1, :], in0=psL[:, 1, :],
                      in1=cst[:, C_ML:C_ML + 128], op=ALU.mult)
                    I("dve", "tensor_copy", [r_g], [r_nl], out=nl[:, 0, :], in_=g_t[:, 0:128])
                    yield
                    I("pe", "matmul", [r_AR, r_lob], [r_py], py[:], Atc, idpad[:], start=True, stop=True)
                    I("pe", "matmul", [r_g, r_v], [r_py], py[:], g_t[:, 256:384], vp_t[:], start=False, stop=True, skip_group_check=True)
                    for lvl in range(6):
                        yc, r_yc = ycr.next()
                        I("act", "activation", [r_py], [r_yc], out=yc[:], in_=py[:], func=ACTF.Copy)
                        I("pe", "matmul", [r_nl, r_yc], [r_py], py[:], nl[:, 0, :], yc[:], start=False, stop=True, skip_group_check=True)
                        if lvl < 5:
                            nl2, r_nl2 = nlr.next()
                            I("pe", "matmul", [r_nl], [r_psL], psL[:, 0, :], nl[:, 1, :], nl[:, 0, :], start=True, stop=True)
                            if lvl < 4:
                                I("pe", "matmul", [r_nl], [r_psL], psL[:, 1, :], nl[:, 0, :], nl[:, 1, :], start=True,
                                  stop=True)
                                I("dve", "tensor_copy", [r_psL], [r_nl2], out=nl2[:], in_=psL[:])
                            else:
                                I("dve", "tensor_copy", [r_psL], [r_nl2], out=nl2[:, 0, :], in_=psL[:, 0, :])
                            nl, r_nl = nl2, r_nl2
                        yield
                    I("act", "activation", [r_py], [r_wt], out=w_t[:], in_=py[:], func=ACTF.Copy)
                    I("pe", "matmul", [r_wt, r_g], [r_ps45], ps45[0:64, 0:128], w_t[:, 0:64], g_t[:, 128:256],
                      start=True, stop=True)
                    I("pe", "matmul", [r_wt, r_bk], [r_ps45], ps45[0:64, 128:256], w_t[:, 0:64], bk_t[:, 0, :],
                      start=True, stop=True)
                    I("pe", "matmul", [r_g, r_wt], [r_ps45], ps45[:, 256:320], g_t[:, 128:256], w_t[:, 64:128],
                      start=True, stop=False)
                    I("pe", "matmul", [r_g, r_v], [r_ps45], ps45[:, 256:320], g_t[:, 384:512], v_t,
                      start=False, stop=True)
                    I("pe", "matmul", [r_bk, r_wt], [r_ps45], ps45[:, 320:384], bk_t[:, 0, :], w_t[:, 64:128],
                      start=True, stop=False)
                    I("pe", "matmul", [r_bk, r_v], [r_ps45], ps45[:, 320:384], bk_t[:, 1, :], v_t,
                      start=False, stop=True)
                    I("pe", "matmul", [r_RK, r_ones], [r_ps45], ps45[0:64, 384:448], RK[:, c, :], ones64,
                      start=True, stop=True)
                    I("pe", "matmul", [r_sgb, r_lob], [r_ps45], ps45[0:64, 448:512], sgb[:, 0, c * 64:(c + 1) * 64],
                      gub[:, 0, h * 64:(h + 1) * 64], start=True, stop=False)
                    I("pe", "matmul", [r_sgb, r_lob], [r_ps45], ps45[0:64, 448:512], sgb[:, 1, c * 64:(c + 1) * 64],
                      gub[:, 1, h * 64:(h + 1) * 64], start=False, stop=True)
                    I("dve", "tensor_tensor", [r_ps45, r_AR], [r_pa], out=pa_t[:, 0:128], in0=ps45[0:64, 0:128],
                      in1=Rtc, op=ALU.add)
                    for d in range(2):
                        I("dve", "scalar_tensor_tensor", [r_cst, r_gam, r_ps45], [r_pa],
                          out=pa_t[:, 128 + d * 64:192 + d * 64], in0=cst[0:64, C_ID:C_ID + 64],
                          scalar=gam[:, d, c:c + 1], in1=ps45[0:64, 128 + d * 64:192 + d * 64], op0=ALU.mult, op1=ALU.add)
                    I("dve", "tensor_copy", [r_ps45], [r_qb], out=qb_t[:], in_=ps45[:, 256:384])
                    I("dve", "tensor_tensor", [r_ps45, r_v], [r_bg], out=bg_t[:, 0:64], in0=ps45[0:64, 384:448],
                      in1=vp_t[0:64, 64:128], op=ALU.mult)
                    I("dve", "tensor_copy", [r_ps45], [r_bg], out=bg_t[:, 64:128], in_=ps45[0:64, 448:512])
                    cg = ch0 + c
                    em.dma("sp", s_pa, PA[h][cg, :, :], pa_t[:], reads=[r_pa], writes=[r_PA[h]])
                    em.dma("sp", s_qb, QB[h][cg, :, :], qb_t[:], reads=[r_qb], writes=[r_QB[h]])
                    em.dma("sp", s_bg, BG[h][cg, :, :], bg_t[:], reads=[r_bg], writes=[r_BG[h]])
                    yield

                nh = DBG.get('p3h', 4)
                nu = min(nc_, DBG.get("units", 99))
                prep(0, 0)
                for h in range(nh):
                    rec = []
                    if h + 1 < nh:
                        REC[0] = rec
                        prep(h + 1, (h + 1) % 2)
                        REC[0] = None
                    npairs = (nu + 1) // 2
                    per = max(1, -(-len(rec) // max(1, npairs * 9)))
                    pos = 0
                    for c0 in range(0, nu, 2):
                        live = [unit_gen(c0 + j, j, h, h % 2) for j in range(min(2, nu - c0))]
                        while live:
                            for g_ in list(live):
                                try:
                                    next(g_)
                                except StopIteration:
                                    live.remove(g_)
                            replay(rec[pos:pos + per])
                            pos += per
                    replay(rec[pos:])
            allr = [r_par, r_lob, r_twd, r_adz, r_sgb, r_zsh, r_rr, r_kk, r_vv, r_sg, r_av, r_kd, r_bv, r_Pc, r_kkn, r_nkk,
                    r_tmpa, r_tmpb, r_kq, r_Ein, r_Eng, r_Eex, r_EC,
                    r_psT, r_psG, r_psL, r_ps45] + r_PA + r_QB + r_BG + [x_ for o_ in opsets for x_ in o_[8:]] + \
                   [x[1] for x in BKbd + zl + psX + psY + Vtm2 + GBK + NL + Yc + Wsb + oPA + oQB + oBG]
            barrier(allr)

        if stop_after >= 4:
          with ExitStack() as ph:
            psb = lambda name, shape, dt=F32: ph.enter_context(nc.sbuf_tensor(name, list(shape), dt))
            pst = lambda name, shape, dt=F32: ph.enter_context(nc.psum_tensor(name, list(shape), dt))
            rbt = psb("rbt", [64, 4, 2, 64])
            r_rbt = R()
            s_rb = em.dsem("rb")
            em.dma("sp", s_rb, rbt[:], rowb[:, :, :, :], writes=[r_rbt])
            ysum = psb("ysum", [64, 2, 128, 64])
            r_y = [R(), R()]
            Hs = [[(psb("H%d_%d" % (d, i), [64, 64]), R()) for i in range(2)] for d in range(2)]
            pab = [[(psb("pab%d_%d" % (d, i), [64, 8, 2, 64]), R(), em.dsem("pab%d_%d" % (d, i))) for i in range(2)]
                   for d in range(2)]
            qbb = [[(psb("qbb%d_%d" % (d, i), [64, 8, 2, 64]), R(), em.dsem("qbb%d_%d" % (d, i))) for i in range(2)]
                   for d in range(2)]
            psH = [(pst("psH%d" % d, [64, 64]), R()) for d in range(2)]
            psYY = [(pst("psYY%d" % d, [64, 64]), R()) for d in range(2)]
            psTr = pst("psTr", [64, 512]); r_psTr = R()
            bgb = psb("bgb", [64, 32, 2, 64]); r_bgb = R(); s_bgb = em.dsem("bgb")
            yt = psb("yt", [64, 32, 64]); r_yt = R()
            yt2 = psb("yt2", [64, 32, 64]); r_yt2 = R()
            st_ = psb("st_", [64, 4, 32]); r_st = R()
            rwst = [(psb("rwst%d" % i, [64, 512], BF16), R(), em.dsem("rwst%d" % i)) for i in range(2)]
            rwi = 0
            for h in range(DBG.get('p3h', 4)):
                batches = {0: [list(range(0, 4))] + [list(range(4 + 8 * k, 12 + 8 * k)) for k in range(16)],
                           1: [[3, 2, 1, 0]] + [list(range(4 + 8 * k + 7, 4 + 8 * k - 1, -1)) for k in range(15, -1, -1)]}
                hcur = [0, 0]
                for d in range(2):
                    I("dve", "memset", [], [Hs[d][0][1]], Hs[d][0][0][:], 0.0)
                for bi in range(17):
                    loaded = {}
                    for d in range(2):
                        ids = batches[d][bi]
                        lo, n = min(ids), len(ids)
                        pa_t, r_pa, s_pa = pab[d][bi % 2]
                        qb_t, r_qb, s_qb = qbb[d][bi % 2]
                        for x_ in range(2):
                            em.dma("sp", s_pa, pa_t[:, 0:n, x_, :],
                                   PA[h][lo:lo + n, :, x_ * 128 + d * 64:x_ * 128 + d * 64 + 64].rearrange("c k j -> k c j"),
                                   reads=[r_PA[h]], writes=[r_pa])
                        em.dma("sp", s_qb, qb_t[:, 0:n, :, :],
                               QB[h][lo:lo + n, d * 64:(d + 1) * 64, :].rearrange("c p (x j) -> p c x j", x=2),
                               reads=[r_QB[h]], writes=[r_qb])
                        loaded[d] = (ids, lo, pa_t, r_pa, qb_t, r_qb)
                    n = len(batches[0][bi])
                    for step in range(n):
                        for d in range(2):
                            ids, lo, pa_t, r_pa, qb_t, r_qb = loaded[d]
                            cid = ids[step]
                            j = cid - lo
                            Hc, r_Hc = Hs[d][hcur[d] % 2]
                            Hn, r_Hn = Hs[d][(hcur[d] + 1) % 2]
                            hcur[d] += 1
                            pH, r_pH = psH[d]
                            if cid >= 4:
                                pY, r_pY = psYY[d]
                                I("pe", "matmul", [r_pa, r_Hc], [r_pY], pY[:], pa_t[:, j, 0, :], Hc[:], start=True, stop=True)
                                I("dve", "tensor_tensor", [r_pY, r_qb], [r_y[d]], out=ysum[:, d, cid - 4, :], in0=pY[:],
                                  in1=qb_t[:, j, 0, :], op=ALU.add)
                            I("pe", "matmul", [r_pa, r_Hc], [r_pH], pH[:], pa_t[:, j, 1, :], Hc[:], start=True, stop=True)
                            I("dve", "tensor_tensor", [r_pH, r_qb], [r_Hn], out=Hn[:], in0=pH[:], in1=qb_t[:, j, 1, :],
                              op=ALU.add)
                for qtr in range(4):
                    c0 = qtr * 32
                    em.dma("sp", s_bgb, bgb[:], BG[h][4 + c0:4 + c0 + 32, :, :].rearrange("c p (x j) -> p c x j", x=2),
                           reads=[r_BG[h]], writes=[r_bgb])
                    I("dve", "tensor_tensor", r_y, [r_yt], out=yt[:], in0=ysum[:, 0, c0:c0 + 32, :],
                      in1=ysum[:, 1, c0:c0 + 32, :], op=ALU.add)
                    I("dve", "reduce_sum", [r_yt], [r_st], out=st_[:, 0, :], in_=yt[:], axis=AX.X)
                    I("dve", "tensor_scalar", [r_st], [r_st], out=st_[:, 0, :], in0=st_[:, 0, :], scalar1=1.0 / 64,
                      scalar2=None, op0=ALU.mult)
                    I("dve", "tensor_tensor", [r_yt, r_st], [r_yt], out=yt[:], in0=yt[:],
                      in1=st_[:, 0, :].unsqueeze(2).broadcast_to([64, 32, 64]), op=ALU.subtract)
                    I("pool", "tensor_tensor", [r_yt], [r_yt2], out=yt2[:], in0=yt[:], in1=yt[:], op=ALU.mult)
                    I("dve", "reduce_sum", [r_yt2], [r_st], out=st_[:, 1, :], in_=yt2[:], axis=AX.X)
                    I("dve", "tensor_scalar", [r_st], [r_st], out=st_[:, 1, :], in0=st_[:, 1, :], scalar1=1.0 / 64,
                      scalar2=64e-5, op0=ALU.mult, op1=ALU.add)
                    I("act", "activation", [r_st], [r_st], out=st_[:, 1, :], in_=st_[:, 1, :], func=ACTF.Sqrt)
                    I("dve", "reciprocal", [r_st], [r_st], out=st_[:, 1, :], in_=st_[:, 1, :])
                    I("dve", "tensor_tensor", [r_yt, r_st], [r_yt], out=yt[:], in0=yt[:],
                      in1=st_[:, 1, :].unsqueeze(2).broadcast_to([64, 32, 64]), op=ALU.mult)
                    I("dve", "tensor_tensor", [r_yt, r_rbt], [r_yt], out=yt[:], in0=yt[:],
                      in1=rbt[:, h, 0, :].unsqueeze(1).broadcast_to([64, 32, 64]), op=ALU.mult)
                    I("dve", "tensor_tensor", [r_yt, r_rbt], [r_yt], out=yt[:], in0=yt[:],
                      in1=rbt[:, h, 1, :].unsqueeze(1).broadcast_to([64, 32, 64]), op=ALU.add)
                    I("dve", "tensor_tensor", [r_yt, r_bgb], [r_yt], out=yt[:], in0=yt[:], in1=bgb[:, :, 0, :], op=ALU.add)
                    I("dve", "tensor_tensor", [r_yt, r_bgb], [r_yt], out=yt[:], in0=yt[:], in1=bgb[:, :, 1, :], op=ALU.mult)
                    for g8 in range(4):
                        rw_t, r_rw, s_rw = rwst[rwi % 2]
                        rwi += 1
                        for j in range(8):
                            I("pe", "transpose", [r_yt, r_cst], [r_psTr], psTr[:, j * 64:(j + 1) * 64], yt[:, g8 * 8 + j, :],
                              ident_f[0:64, 0:64])
                        I("act", "activation", [r_psTr], [r_rw], out=rw_t[:], in_=psTr[:], func=ACTF.Copy)
                        tok0 = (c0 + g8 * 8) * 64
                        qd, col = tok0 // TOK, tok0 % TOK
                        em.dma("sp", s_rw, mixL[qd * 512 + 256 + h * 64:qd * 512 + 256 + h * 64 + 64, col:col + 512], rw_t[:],
                               reads=[r_rw], writes=[r_mixL])
            barrier([r_mixL, r_rbt, r_bgb, r_yt, r_yt2, r_st, r_psTr] + r_y +
                    [x[1] for x in Hs[0] + Hs[1] + pab[0] + pab[1] + qbb[0] + qbb[1] + psH + psYY + rwst])

        if debug:
            mixD = nc.dram_tensor("dbg_mixL", [4 * 512, TOK], BF16, kind="ExternalOutput")
            em.dma("sp", s_dbg, mixD.ap(), mixL.ap(), reads=[r_mixL], writes=[r_dbg])
        if stop_after >= 5:
            s_cc = em.sems["cc"] = nc.alloc_semaphore("cc")
            em.wait_all("pool", [r_mixL])
            for k in range(8):
                em.raw("pool", lambda e, k=k: e.collective_compute(
                    "AllGather", ALU.bypass, replica_groups=[[0, 1, 2, 3], [4, 5, 6, 7]],
                    ins=[mixL[k * 256:(k + 1) * 256, :]], outs=[mixG[k * 1024:(k + 1) * 1024, :]]).then_inc(s_cc))
            em.raw("pool", lambda e: e.wait_ge(s_cc, 8))
            I("pool", "memset", [], [r_mixG], ones_bf[0:1, 0:1], 1.0)

        if stop_after >= 6:
          with ExitStack() as ph:
            psb = lambda name, shape, dt=F32: ph.enter_context(nc.sbuf_tensor(name, list(shape), dt))
            pst = lambda name, shape, dt=F32: ph.enter_context(nc.psum_tensor(name, list(shape), dt))
            mixT = psb("mixT", [128, KC, 512], BF16); r_mixT = R(); s_mix = em.dsem("mix")
            x1 = psb("x1", [128, KC, 512]); r_x1 = R(); s_x1 = em.dsem("x1")
            sq = psb("sq2", [128, KC, 512], BF16); r_sq = R()
            h2, r_h2 = mixT, r_mixT
            aT = psb("aT", [128, 44, 512], BF16); r_aT = R()
            rstd = psb("rstd2", [128, 512]); r_rstd = R()
            su = psb("su", [128, 512]); r_su = R()
            wf = [(psb("wf%d" % i, [128, 16, 128]), R(), em.dsem("wf%d" % i)) for i in range(6)]
            wfrot = Rot(wf)
            wbf = [(psb("wbf%d" % i, [128, 16, 128], BF16), R()) for i in range(6)]
            wbrot = Rot(wbf)
            ost = [(psb("ost6_%d" % i, [128, 512]), R(), em.dsem("ost6_%d" % i)) for i in range(2)]
            orot = Rot(ost)
            pss = [(pst("ps6_%d" % i, [128, 512]), R()) for i in range(4)]
            psrot = Rot(pss)
            ps_ss = pst("ps_ss6", [128, 512]); r_psss = R()
            gcache = {}

            def gi():
                if "v" not in gcache:
                    gcache["v"] = nc.gpsimd.partition_id() % 4
                return gcache["v"]
            wo_v = w_out.ap().rearrange("(kc p) n -> p kc n", p=128)
            w1_v = w1.ap().rearrange("(kc p) n -> p kc n", p=128)
            w3_v = w3.ap().rearrange("(kc p) n -> p kc n", p=128)
            w2_v = w2.ap().rearrange("(kc p) n -> p kc n", p=128)
            xk_v = xtok.ap().rearrange("(kc p) t -> p kc t", p=128)
            mg_v = mixG.ap()
            o_v = outT.ap().rearrange("(kc p) t -> p kc t", p=128)
            cast_i = [0]

            def load_w(view, nk, c0, k0=0):
                w_t, r_w, s_w = wfrot.next()
                wb_t, r_wb = wbrot.next()
                em.dma("sp", s_w, w_t[:, 0:nk, :], view[:, k0:k0 + nk, c0:c0 + 128], writes=[r_w])
                eng = ("pool", "pool", "act")[cast_i[0] % 3]
                cast_i[0] += 1
                if eng == "act":
                    I("act", "activation", [r_w], [r_wb], out=wb_t[:, 0:nk, :], in_=w_t[:, 0:nk, :], func=ACTF.Copy)
                else:
                    I(eng, "tensor_copy", [r_w], [r_wb], out=wb_t[:, 0:nk, :], in_=w_t[:, 0:nk, :])
                return wb_t, r_wb

            def norm_stats():
                for kc in range(KC):
                    I("pe", "matmul", [r_sq, r_ones], [r_psss], ps_ss[:], ones_bf[:], sq[:, kc, :], start=(kc == 0),
                      stop=(kc == KC - 1))
                I("dve", "tensor_scalar", [r_psss], [r_rstd], out=rstd[:], in0=ps_ss[:], scalar1=1.0 / D, scalar2=1e-6,
                  op0=ALU.mult, op1=ALU.add)
                I("act", "activation", [r_rstd], [r_rstd], out=rstd[:], in_=rstd[:], func=ACTF.Sqrt)
                I("dve", "reciprocal", [r_rstd], [r_rstd], out=rstd[:], in_=rstd[:])

            for tt in range(4):
                for hf in range(2):
                    em.dma("pool", s_mix, mixT[:, hf * 8:(hf + 1) * 8, :],
                           (lambda tt=tt, hf=hf: mixG[bass.ds(gi() * 2048 + hf * 1024, 1024), tt * 512:(tt + 1) * 512]
                            .rearrange("(k p) t -> p k t", p=128)),
                           reads=[r_mixG], writes=[r_mixT])
                for half in range(2):
                    em.dma("sp", s_x1, x1[:, half * 8:(half + 1) * 8, :],
                           xk_v[:, half * 8:(half + 1) * 8, tt * 512:(tt + 1) * 512], writes=[r_x1])
                for n in range(KC):
                    wb_t, r_wb = load_w(wo_v, KC, n * 128)
                    ps, r_ps = psrot.next()
                    for kc in range(KC):
                        I("pe", "matmul", [r_wb, r_mixT], [r_ps], ps[:], wb_t[:, kc, :], mixT[:, kc, :], start=(kc == 0),
                          stop=(kc == KC - 1))
                    I("dve", "scalar_tensor_tensor", [r_ps, r_modT, r_x1], [r_x1], out=x1[:, n, :], in0=ps[:],
                      scalar=modT[:, 32 + n, 0:1], in1=x1[:, n, :], op0=ALU.mult, op1=ALU.add)
                    I("act", "activation", [r_x1], [r_sq], out=sq[:, n, :], in_=x1[:, n, :], func=ACTF.Square)
                norm_stats()
                for kc in range(KC):
                    I("dve", "tensor_tensor", [r_x1, r_rstd], [r_su], out=su[:], in0=x1[:, kc, :], in1=rstd[:], op=ALU.mult)
                    I("act", "activation", [r_su, r_A2, r_modT], [r_h2], out=h2[:, kc, :], in_=su[:], func=ACTF.Identity,
                      scale=A2[:, kc:kc + 1], bias=modT[:, 48 + kc, 0:1])
                for f in range(44):
                    w1b, r_w1b = load_w(w1_v, KC, f * 128)
                    w3b, r_w3b = load_w(w3_v, KC, f * 128)
                    p1, r_p1 = psrot.next()
                    p3, r_p3 = psrot.next()
                    for kc in range(KC):
                        I("pe", "matmul", [r_w1b, r_h2], [r_p1], p1[:], w1b[:, kc, :], h2[:, kc, :], start=(kc == 0),
                          stop=(kc == KC - 1))
                    for kc in range(KC):
                        I("pe", "matmul", [r_w3b, r_h2], [r_p3], p3[:], w3b[:, kc, :], h2[:, kc, :], start=(kc == 0),
                          stop=(kc == KC - 1))
                    I("act", "activation", [r_p1], [r_su], out=su[:], in_=p1[:], func=ACTF.Silu)
                    I("dve", "tensor_tensor", [r_su, r_p3], [r_aT], out=aT[:, f, :], in0=su[:], in1=p3[:], op=ALU.mult)
                for n in range(KC):
                    ps, r_ps = psrot.next()
                    for (k0, nk) in ((0, 16), (16, 16), (32, 12)):
                        wb_t, r_wb = load_w(w2_v, nk, n * 128, k0)
                        for f in range(nk):
                            I("pe", "matmul", [r_wb, r_aT], [r_ps], ps[:], wb_t[:, f, :], aT[:, k0 + f, :],
                              start=(k0 + f == 0), stop=(k0 + f == 43))
                    I("dve", "scalar_tensor_tensor", [r_ps, r_modT, r_x1], [r_x1], out=x1[:, n, :], in0=ps[:],
                      scalar=modT[:, 80 + n, 0:1], in1=x1[:, n, :], op0=ALU.mult, op1=ALU.add)
                    I("act", "activation", [r_x1], [r_sq], out=sq[:, n, :], in_=x1[:, n, :], func=ACTF.Square)
                norm_stats()
                for kc in range(KC):
                    o_t, r_o, s_o = orot.next()
                    I("dve", "scalar_tensor_tensor", [r_x1, r_cst, r_rstd], [r_o], out=o_t[:], in0=x1[:, kc, :],
                      scalar=ppt[:, P_FG + kc:P_FG + kc + 1], in1=rstd[:], op0=ALU.mult, op1=ALU.mult)
                    em.dma("sp", s_o, o_v[:, kc, tt * 512:(tt + 1) * 512], o_t[:], reads=[r_o], writes=[r_dbg])
            barrier([r_dbg, r_x1, r_mixT, r_sq, r_h2, r_aT, r_rstd, r_su, r_psss] +
                    [x[1] for x in wf + wbf + ost + pss])

        if DBG.get('dummy'):
            dmy = top.enter_context(nc.sbuf_tensor("dmy", [128, 8], F32))
            r_dmy = R()
            for i in range(DBG['dummy']):
                em.ops["pool"].append(lambda e: e.memset(dmy[:], 1.0))
        barrier([r_dbg])
        with nc.Block() as block:
            em.run(block)
    return nc


def _consts():
    c = np.zeros((128, NCONST), np.float32)
    c[:, C_ID:C_ID + 128] = np.eye(128, dtype=np.float32)
    q = np.arange(128)[:, None]
    k = np.arange(128)[None, :]
    c[:, C_MP:C_MP + 128] = np.where(k >= q, 0.0, NEG)
    c[:, C_MN:C_MN + 128] = np.where(k <= q, 0.0, NEG)
    c[:, C_MF:C_MF + 128] = NEG
    s = np.arange(64)[:, None]
    t = np.arange(64)[None, :]
    strictN = np.zeros((128, 128), np.float32)
    inclN = np.zeros((128, 128), np.float32)
    strictN[0:64, 0:64] = (s < t)
    strictN[64:128, 64:128] = (s > t)
    inclN[0:64, 0:64] = (s <= t)
    inclN[64:128, 64:128] = (s >= t)
    c[:, C_MBK:C_MBK + 512] = np.concatenate([strictN, inclN, strictN, inclN], axis=1)
    c[:, C_ML:C_ML + 128] = strictN.T
    rs = np.ones((128, 512), np.float32)
    rs[:, ::64] = 0.0
    c[:, C_RS:C_RS + 512] = rs
    return c


def _rope_tables():
    nf = 16
    freqs = (1.0 / (10000.0 ** (np.arange(nf, dtype=np.float32) / nf))).astype(np.float32)
    t = np.arange(SEQ)
    row = (t // 64).astype(np.float32)
    col = (t % 64).astype(np.float32)
    tab = np.zeros((64, 2, SEQ), np.float32)
    for j in range(64):
        pos = row if j < 32 else col
        ang = pos * freqs[j % 16]
        sign = -1.0 if (j % 32) < 16 else 1.0
        tab[j, 0] = np.cos(ang)
        tab[j, 1] = sign * np.sin(ang)
    return tab


def _prep_inputs(inp):
    f = lambda a: np.ascontiguousarray(a, dtype=np.float32)
    x, c, ctx, c_ctx = inp["x"], inp["c"], inp["ctx"], inp["c_ctx"]
    w_in = inp["w_in"][0]
    perm64 = np.concatenate([np.arange(16, 32), np.arange(0, 16), np.arange(48, 64), np.arange(32, 48)])
    consts = _consts()
    rope = _rope_tables()
    wo = inp["w_out"][0]
    rows = np.concatenate([np.concatenate([np.arange(r * 256, (r + 1) * 256), 1024 + np.arange(r * 256, (r + 1) * 256)])
                           for r in range(4)])
    wo_p = f(wo)
    w1, w3, w2 = f(inp["ffn_w1"][0]), f(inp["ffn_w3"][0]), f(inp["ffn_w2"][0])
    ada_w = f(inp["ada_w"][0])
    ada_bT = f(inp["ada_b"][0].reshape(96, 128).T)
    n1g = f(inp["norm1_g"][0].reshape(KC, 128).T)
    xT_b = [f(np.concatenate([ctx[b], x[b]], axis=0).T) for b in range(2)]
    tsp, tsn = inp["ts_prev"][0], inp["ts_next"][0]
    maps = []
    for core in range(8):
        b, g = core // 4, core % 4
        cols = []
        qc = np.arange(g * 256, (g + 1) * 256)
        cols.append(qc)
        cols.append((qc.reshape(4, 64)[:, perm64]).reshape(-1))
        kc_ = 1024 + np.arange(g * 64, (g + 1) * 64)
        cols.append(kc_)
        cols.append(kc_[perm64])
        cols.append(1280 + np.arange(g * 64, (g + 1) * 64))
        base = 1536
        for i in range(3):
            cols.append(base + i * 1024 + np.arange(g * 256, (g + 1) * 256))
        cols.append(base + 3072 + np.arange(0, 416))
        cols = np.concatenate(cols)
        pp = np.zeros((128, NPP), np.float32)
        pp[:, P_SINK:P_SINK + 4] = inp["attn_sink"][0][g * 4:(g + 1) * 4][None, :]
        grp_idx = []
        for i in range(3):
            for h in range(4):
                grp_idx.append(i * 1024 + g * 256 + h * 64 + np.arange(64))
        for i in range(4):
            grp_idx.append(3072 + i * 64 + np.arange(64))
        grp_idx.append(3328 + np.arange(128))
        grp_idx.append(3328 + 128 + np.arange(32))
        for gi_, idx in enumerate(grp_idx):
            pp[0:len(idx), P_TS + gi_] = tsp[idx]
            pp[0:len(idx), P_TS + 18 + gi_] = tsn[idx]
        hpar = [inp["w0"][0][0], inp["w0"][0][1], inp["a0"][0][0], inp["a0"][0][1], inp["k_k"][0], inp["k_a"][0],
                inp["r_k"][0]]
        for i, arr in enumerate(hpar):
            for h in range(4):
                pp[0:64, P_HP + i * 4 + h] = arr[g * 256 + h * 64:g * 256 + (h + 1) * 64]
        pp[:, P_N2:P_N2 + 16] = inp["norm2_g"][0].reshape(KC, 128).T
        pp[:, P_FG:P_FG + 16] = inp["final_norm_g"].reshape(KC, 128).T
        rowb = np.zeros((64, 4, 2, 64), np.float32)
        for h in range(4):
            rowb[:, h, 0, :] = inp["lnx_g"][0][g * 256 + h * 64:g * 256 + (h + 1) * 64][None, :]
            rowb[:, h, 1, :] = inp["lnx_b"][0][g * 256 + h * 64:g * 256 + (h + 1) * 64][None, :]
        lora = np.zeros((64, 2, 2, 4, 64), np.float32)
        for d in range(2):
            lora[:, 0, d] = inp["w_up"][0][d][:, g * 256:(g + 1) * 256].reshape(64, 4, 64)
            lora[:, 1, d] = inp["a_up"][0][d][:, g * 256:(g + 1) * 256].reshape(64, 4, 64)
        m = {
            "xT": xT_b[b],
            "xtok": f(xT_b[b][:, CTX + g * TOK:CTX + (g + 1) * TOK]),
            "c_col": f(np.stack([c[b], c_ctx], axis=1).reshape(KC, 128, 2).transpose(1, 0, 2)),
            "ada_w": ada_w, "ada_bT": ada_bT, "n1g": n1g,
            "Wc": f(w_in[:, cols]),
            "consts": consts, "pp": pp, "rope": rope, "rowb": rowb, "lora": lora,
            "gup": f(inp["g_up"][0][:, g * 256:(g + 1) * 256]),
            "w_out": wo_p, "w1": w1, "w3": w3, "w2": w2,
        }
        maps.append(m)
    return maps


_NC_CACHE = {}


def kernel(**inputs):
    inputs = {k: np.asarray(v) for k, v in inputs.items()}
    maps = _prep_inputs(inputs)
    if "nc" not in _NC_CACHE:
        _NC_CACHE["nc"] = build_program()
    nc = _NC_CACHE["nc"]
    res = run_bass_kernel_spmd(nc, maps, core_ids=list(range(8)))
    out = np.zeros((2, SEQ, D), np.float32)
    for core in range(8):
        b, g = core // 4, core % 4
        out[b, g * TOK:(g + 1) * TOK, :] = np.asarray(res.results[core]["outT"]).T
    return out
```
